# Optimizing a Trainium2 kernel written in Bass

```python
import math
import jax, jax.numpy as jnp
from jax import lax
import numpy as np

D_MODEL = 1024
BATCH = 32
SEQ = 2048
DEPTH = 2
DEC_BATCH = 8
DEC_SEQ = 2048
PAST_LEN = 128

GRID_W = 64
HEAD_DIM = 64
QBLK = 128
ROPE_THETA = 10000.0
EPS = 1e-6
N_BRANCH = 4
A_HEADS = 4
A_D = HEAD_DIM // 2
A_VD = HEAD_DIM
B_HEADS = 4
B_D = HEAD_DIM
WIN_R = 8
WIN_C = 16
B_QCB = WIN_C
B_KCB = 2 * WIN_C
C_HEADS = 4
C_NOPE = 64
C_ROPE = 32
C_VD = 64
C_QLORA = 192
C_KVLORA = 128
D_HEADS = 4
D_KV_HEADS = 2
D_D = HEAD_DIM
D_GROUP = D_HEADS // D_KV_HEADS
BRANCH_W = 256
A_QK_COLS = A_HEADS * 2 * A_D
A_COLS = 2 * A_QK_COLS + A_HEADS * A_VD
B_COLS = 3 * B_HEADS * B_D
C_COLS = C_QLORA + C_KVLORA + C_ROPE
D_COLS = (D_HEADS + 2 * D_KV_HEADS) * D_D
GATE_COLS = N_BRANCH * D_MODEL
IN_COLS = A_COLS + B_COLS + C_COLS + D_COLS + GATE_COLS
FFN_DIM = 2816
CONV_W = 3

kernel_name = "hybrid_gated_parallel_encoder"


def rmsnorm(x, g):
    x32 = x.astype(jnp.float32)
    y = x32 * lax.rsqrt(jnp.mean(x32 * x32, axis=-1, keepdims=True) + EPS)
    return (y * g.astype(jnp.float32)).astype(x.dtype)


def rope_angles(pos, dim):
    inv_freq = ROPE_THETA ** (-jnp.arange(0, dim, 2, dtype=jnp.float32) / dim)
    return pos.astype(jnp.float32)[:, None] * inv_freq[None, :]


def apply_rope(x, ang):
    half = x.shape[-1] // 2
    shape = (1, ang.shape[0]) + (1,) * (x.ndim - 3) + (half,)
    cos = jnp.cos(ang).reshape(shape).astype(x.dtype)
    sin = jnp.sin(ang).reshape(shape).astype(x.dtype)
    x1, x2 = x[..., :half], x[..., half:]
    return jnp.concatenate([x1 * cos - x2 * sin, x2 * cos + x1 * sin], axis=-1)


def axial_rope(x, ang_row, ang_col):
    half = x.shape[-1] // 2
    return jnp.concatenate([apply_rope(x[..., :half], ang_row), apply_rope(x[..., half:], ang_col)], axis=-1)


def to_query_blocks(a):
    b, s = a.shape[:2]
    a = a.reshape((b, s // QBLK, QBLK) + a.shape[2:])
    return jnp.moveaxis(a, 1, 0)


def from_query_blocks(o):
    o = jnp.moveaxis(o, 0, 1)
    return o.reshape((o.shape[0], o.shape[1] * o.shape[2]) + o.shape[3:])


def diff_mixer(cols, ang, lq1, lk1, lq2, lk2, subln, lam_init):
    b, s, _ = cols.shape
    f32 = jnp.float32
    q, k, v = jnp.split(cols, [A_QK_COLS, 2 * A_QK_COLS], axis=-1)
    q = apply_rope(q.reshape(b, s, A_HEADS, 2, A_D), ang) * (A_D ** -0.5)
    k = apply_rope(k.reshape(b, s, A_HEADS, 2, A_D), ang)
    v = v.reshape(b, s, A_HEADS, A_VD)
    lam = (jnp.exp(jnp.sum(lq1.astype(f32) * lk1.astype(f32)))
           - jnp.exp(jnp.sum(lq2.astype(f32) * lk2.astype(f32))) + lam_init)

    def block(qb):
        sc = jnp.einsum('bqhcd,bkhcd->bhcqk', qb, k).astype(f32)
        p = jax.nn.softmax(sc, axis=-1)
        w = (p[:, :, 0] - lam * p[:, :, 1]).astype(v.dtype)
        return jnp.einsum('bhqk,bkhd->bqhd', w, v)

    o = from_query_blocks(lax.map(block, to_query_blocks(q)))
    o = rmsnorm(o, subln) * (1.0 - lam_init)
    return o.reshape(b, s, A_HEADS * A_VD)


def natten_mixer(cols, rpb):
    b, s, _ = cols.shape
    hd = B_HEADS * B_D
    q, k, v = jnp.split(cols, [hd, 2 * hd], axis=-1)
    rows = s // GRID_W
    wr = min(WIN_R, rows)
    ncb = GRID_W // B_QCB
    qg = (q * (B_D ** -0.5)).reshape(b, rows, ncb, B_QCB, B_HEADS, B_D)
    kg = k.reshape(b, rows, GRID_W, B_HEADS, B_D)
    vg = v.reshape(b, rows, GRID_W, B_HEADS, B_D)
    cb = np.clip(np.arange(ncb) * B_QCB - WIN_C // 2, 0, GRID_W - B_KCB)
    col_idx = cb[:, None] + np.arange(B_KCB)
    qcol = np.arange(GRID_W).reshape(ncb, B_QCB)
    cs = np.clip(qcol - WIN_C // 2, 0, GRID_W - WIN_C)
    kc = col_idx[:, None, :]
    col_mask = (kc >= cs[..., None]) & (kc < cs[..., None] + WIN_C)
    dc_idx = np.clip(kc - qcol[..., None] + WIN_C - 1, 0, 2 * WIN_C - 2)
    n_keys = wr * B_KCB
    mask = np.broadcast_to(col_mask[:, :, None, :], (ncb, B_QCB, wr, B_KCB)).reshape(ncb, B_QCB, n_keys)
    mask = jnp.asarray(mask)
    dc_b = jnp.asarray(dc_idx[:, :, None, :])

    def row_block(r):
        rs = jnp.clip(r - wr // 2, 0, rows - wr)
        k_blk = lax.dynamic_slice_in_dim(kg, rs, wr, axis=1)[:, :, col_idx]
        v_blk = lax.dynamic_slice_in_dim(vg, rs, wr, axis=1)[:, :, col_idx]
        k_blk = jnp.moveaxis(k_blk, 1, 2).reshape(b, ncb, n_keys, B_HEADS, B_D)
        v_blk = jnp.moveaxis(v_blk, 1, 2).reshape(b, ncb, n_keys, B_HEADS, B_D)
        q_blk = lax.dynamic_index_in_dim(qg, r, axis=1, keepdims=False)
        sc = jnp.einsum('bnqhd,bnlhd->bhnql', q_blk, k_blk).astype(jnp.float32)
        dr_idx = rs + jnp.arange(wr) - r + WIN_R - 1
        bias = rpb[:, dr_idx[None, None, :, None], dc_b]
        bias = bias.reshape(B_HEADS, ncb, B_QCB, n_keys).astype(jnp.float32)
        p = jax.nn.softmax(jnp.where(mask, sc + bias, -jnp.inf), axis=-1)
        o = jnp.einsum('bhnql,bnlhd->bnqhd', p.astype(v_blk.dtype), v_blk)
        return o.reshape(b, GRID_W, B_HEADS, B_D)

    out = lax.map(row_block, jnp.arange(rows))
    return jnp.moveaxis(out, 0, 1).reshape(b, s, hd)


def mla_mixer(cols, ang, q_norm, kv_norm, w_uq, w_ukv):
    b, s, _ = cols.shape
    cq, ckv, k_rope = jnp.split(cols, [C_QLORA, C_QLORA + C_KVLORA], axis=-1)
    q = (rmsnorm(cq, q_norm) @ w_uq).reshape(b, s, C_HEADS, C_NOPE + C_ROPE)
    q = jnp.concatenate([q[..., :C_NOPE], apply_rope(q[..., C_NOPE:], ang)], axis=-1)
    q = q * ((C_NOPE + C_ROPE) ** -0.5)
    kv = (rmsnorm(ckv, kv_norm) @ w_ukv).reshape(b, s, C_HEADS, C_NOPE + C_VD)
    k_nope, v = kv[..., :C_NOPE], kv[..., C_NOPE:]
    k_rope = apply_rope(k_rope, ang)[:, :, None, :]
    k = jnp.concatenate([k_nope, jnp.broadcast_to(k_rope, (b, s, C_HEADS, C_ROPE))], axis=-1)

    def block(qb):
        sc = jnp.einsum('bqhd,bkhd->bhqk', qb, k).astype(jnp.float32)
        p = jax.nn.softmax(sc, axis=-1).astype(v.dtype)
        return jnp.einsum('bhqk,bkhd->bqhd', p, v)

    o = from_query_blocks(lax.map(block, to_query_blocks(q)))
    return o.reshape(b, s, C_HEADS * C_VD)


def gqa_mixer(cols, ang_row, ang_col, q_norm, k_norm):
    b, s, _ = cols.shape
    qd = D_HEADS * D_D
    kvd = D_KV_HEADS * D_D
    q, k, v = jnp.split(cols, [qd, qd + kvd], axis=-1)
    q = axial_rope(rmsnorm(q.reshape(b, s, D_KV_HEADS, D_GROUP, D_D), q_norm), ang_row, ang_col) * (D_D ** -0.5)
    k = axial_rope(rmsnorm(k.reshape(b, s, D_KV_HEADS, D_D), k_norm), ang_row, ang_col)
    v = v.reshape(b, s, D_KV_HEADS, D_D)

    def block(qb):
        sc = jnp.einsum('bqngd,bknd->bngqk', qb, k).astype(jnp.float32)
        p = jax.nn.softmax(sc, axis=-1).astype(v.dtype)
        return jnp.einsum('bngqk,bknd->bqngd', p, v)

    o = from_query_blocks(lax.map(block, to_query_blocks(q)))
    return o.reshape(b, s, qd)


def conv_ffn(h, w_in, conv_w, conv_b, w_out):
    ug = h @ w_in
    u, g = ug[..., :FFN_DIM], ug[..., FFN_DIM:]
    gp = jnp.pad(g, ((0, 0), (1, 1), (0, 0)))
    g = gp[:, :-2] * conv_w[0] + gp[:, 1:-1] * conv_w[1] + gp[:, 2:] * conv_w[2] + conv_b
    return (jax.nn.silu(g) * u) @ w_out


def encoder_trunk(x, attn_norm, w_in, a_lambda_q1, a_lambda_k1, a_lambda_q2, a_lambda_k2, a_subln,
                  b_rpb, c_q_norm, c_kv_norm, c_w_uq, c_w_ukv, d_q_norm, d_k_norm,
                  w_branch, w_out, ffn_norm, w_ffn_in, ffn_conv_w, ffn_conv_b, w_ffn_out, final_norm):
    s = x.shape[1]
    t = jnp.arange(s)
    ang_a = rope_angles(t, A_D)
    ang_c = rope_angles(t, C_ROPE)
    ang_row = rope_angles(t // GRID_W, D_D // 2)
    ang_col = rope_angles(t % GRID_W, D_D // 2)
    p1 = A_COLS
    p2 = p1 + B_COLS
    p3 = p2 + C_COLS
    p4 = p3 + D_COLS
    for l in range(DEPTH):
        lam_init = 0.8 - 0.6 * math.exp(-0.3 * l)
        h = rmsnorm(x, attn_norm[l])
        proj = h @ w_in[l]
        a_cols, b_cols, c_cols, d_cols, g_cols = jnp.split(proj, [p1, p2, p3, p4], axis=-1)
        o_a = diff_mixer(a_cols, ang_a, a_lambda_q1[l], a_lambda_k1[l], a_lambda_q2[l], a_lambda_k2[l],
                         a_subln[l], lam_init)
        o_b = natten_mixer(b_cols, b_rpb[l])
        o_c = mla_mixer(c_cols, ang_c, c_q_norm[l], c_kv_norm[l], c_w_uq[l], c_w_ukv[l])
        o_d = gqa_mixer(d_cols, ang_row, ang_col, d_q_norm[l], d_k_norm[l])
        gates = jax.nn.sigmoid(g_cols)
        merged = None
        for i, o in enumerate((o_a, o_b, o_c, o_d)):
            term = gates[..., i * D_MODEL:(i + 1) * D_MODEL] * (o @ w_branch[l, i])
            merged = term if merged is None else merged + term
        x = x + merged @ w_out[l]
        x = x + conv_ffn(rmsnorm(x, ffn_norm[l]), w_ffn_in[l], ffn_conv_w[l], ffn_conv_b[l], w_ffn_out[l])
    return rmsnorm(x, final_norm)


def setup_inputs(seed: int = 0) -> dict:
    key = jax.random.key(seed)
    ks = jax.random.split(key, 24)
    f32 = jnp.float32

    def nrm(k, shape, scale):
        return jax.random.normal(k, shape, f32) * scale

    def gain(k, shape):
        return 1.0 + 0.02 * jax.random.normal(k, shape, f32)

    return {
        "x_prompt": nrm(ks[0], (BATCH, SEQ, D_MODEL), 1.0),
        "x_sample": nrm(ks[1], (DEC_BATCH, DEC_SEQ, D_MODEL), 1.0),
        "attn_norm": gain(ks[2], (DEPTH, D_MODEL)),
        "w_in": nrm(ks[3], (DEPTH, D_MODEL, IN_COLS), D_MODEL ** -0.5),
        "a_lambda_q1": nrm(ks[4], (DEPTH, A_D), 0.1),
        "a_lambda_k1": nrm(ks[5], (DEPTH, A_D), 0.1),
        "a_lambda_q2": nrm(ks[6], (DEPTH, A_D), 0.1),
        "a_lambda_k2": nrm(ks[7], (DEPTH, A_D), 0.1),
        "a_subln": gain(ks[8], (DEPTH, A_VD)),
        "b_rpb": nrm(ks[9], (DEPTH, B_HEADS, 2 * WIN_R - 1, 2 * WIN_C - 1), 0.02),
        "c_q_norm": gain(ks[10], (DEPTH, C_QLORA)),
        "c_kv_norm": gain(ks[11], (DEPTH, C_KVLORA)),
        "c_w_uq": nrm(ks[12], (DEPTH, C_QLORA, C_HEADS * (C_NOPE + C_ROPE)), C_QLORA ** -0.5),
        "c_w_ukv": nrm(ks[13], (DEPTH, C_KVLORA, C_HEADS * (C_NOPE + C_VD)), C_KVLORA ** -0.5),
        "d_q_norm": gain(ks[14], (DEPTH, D_D)),
        "d_k_norm": gain(ks[15], (DEPTH, D_D)),
        "w_branch": nrm(ks[16], (DEPTH, N_BRANCH, BRANCH_W, D_MODEL), BRANCH_W ** -0.5),
        "w_out": nrm(ks[17], (DEPTH, D_MODEL, D_MODEL), D_MODEL ** -0.5),
        "ffn_norm": gain(ks[18], (DEPTH, D_MODEL)),
        "w_ffn_in": nrm(ks[19], (DEPTH, D_MODEL, 2 * FFN_DIM), D_MODEL ** -0.5),
        "ffn_conv_w": nrm(ks[20], (DEPTH, CONV_W, FFN_DIM), CONV_W ** -0.5),
        "ffn_conv_b": nrm(ks[21], (DEPTH, FFN_DIM), 0.01),
        "w_ffn_out": nrm(ks[22], (DEPTH, FFN_DIM, D_MODEL), FFN_DIM ** -0.5),
        "final_norm": gain(ks[23], (D_MODEL,)),
    }


def reference(x_prompt, x_sample, attn_norm, w_in, a_lambda_q1, a_lambda_k1, a_lambda_q2, a_lambda_k2,
              a_subln, b_rpb, c_q_norm, c_kv_norm, c_w_uq, c_w_ukv, d_q_norm, d_k_norm,
              w_branch, w_out, ffn_norm, w_ffn_in, ffn_conv_w, ffn_conv_b, w_ffn_out, final_norm):
    params = (attn_norm, w_in, a_lambda_q1, a_lambda_k1, a_lambda_q2, a_lambda_k2, a_subln,
              b_rpb, c_q_norm, c_kv_norm, c_w_uq, c_w_ukv, d_q_norm, d_k_norm,
              w_branch, w_out, ffn_norm, w_ffn_in, ffn_conv_w, ffn_conv_b, w_ffn_out, final_norm)
    y_prompt = encoder_trunk(x_prompt, *params)
    y_sample = encoder_trunk(x_sample, *params)
    return (y_prompt, y_sample)
```

```python
import math
from contextlib import ExitStack

import numpy as np
import concourse.bass as bass
import concourse.mybir as mybir
from concourse.bass_utils import run_bass_kernel_spmd

F32 = mybir.dt.float32
BF16 = mybir.dt.bfloat16
AF = mybir.ActivationFunctionType
ALU = mybir.AluOpType
AX = mybir.AxisListType

S = 2048
D = 1024
NT = 16
EPS = 1e-6
FFN = 2816
NFC = 22
N_CORES = 8
SEQ_PER_CORE = 5

A0 = 0
B0 = 1280
C0 = 2048
D0 = 2560
G0 = 3712
NMIX = G0 + 4096

PV_DQ, PV_DQP, PV_DK, PV_DKP, PV_SUB, PV_CQ0, PV_CQ1, PV_CKV, PV_CONV = 0, 1, 2, 3, 4, 5, 6, 7, 8
PV_L = 8 + 4 * NFC


class _Op:
    __slots__ = ("eng", "fn", "deps", "marked", "idx", "dma_sem", "dma_val", "semval", "rw")


class Sched:
    ENG = ("pe", "act", "dve", "pool", "sp")

    def __init__(self):
        self.ops = {e: [] for e in self.ENG}
        self.state = {}
        self.dma_count = {}
        self.pending = {e: [] for e in self.ENG}
        self.last_dma = {}
        self.phase = ""
        self.pe_phase = []

    def add(self, eng, fn, reads=(), writes=(), dma_sem=None):
        op = _Op()
        op.eng, op.fn, op.marked, op.dma_sem = eng, fn, False, dma_sem
        op.dma_val = 0
        op.semval = 0
        op.rw = (tuple(reads), tuple(writes))
        deps = {}

        def dep(d):
            if d is None:
                return
            if d.dma_sem is not None:
                k = ("dma", d.dma_sem)
                if k not in deps or deps[k].dma_val < d.dma_val:
                    deps[k] = d
            else:
                if d.eng == "pe" and eng == "pe" and dma_sem is None:
                    return
                k = d.eng
                if k not in deps or deps[k].idx < d.idx:
                    deps[k] = d

        st_ = self.state
        for k in reads:
            s = st_.get(k)
            if s is not None:
                dep(s[0])
        for k in writes:
            s = st_.get(k)
            if s is not None:
                dep(s[0])
                for r in s[1].values():
                    dep(r)
                for r in s[2]:
                    dep(r)
        for d in self.pending[eng]:
            dep(d)
        self.pending[eng] = []
        if dma_sem is not None:
            c = self.dma_count.get(dma_sem, 0) + 1
            self.dma_count[dma_sem] = c
            op.dma_val = 16 * c
            self.last_dma[dma_sem] = op
        op.idx = len(self.ops[eng])
        self.ops[eng].append(op)
        if eng == "pe":
            self.pe_phase.append(self.phase)
        op.deps = list(deps.values())
        for d in op.deps:
            if d.dma_sem is None:
                d.marked = True
        for k in writes:
            st_[k] = [op, {}, []]
        for k in reads:
            s = st_.get(k)
            if s is None:
                s = [None, {}, []]
                st_[k] = s
            if dma_sem is not None:
                s[2].append(op)
            else:
                s[1][eng] = op
        return op

    def barrier(self):
        lasts = []
        for e in self.ENG:
            for op in reversed(self.ops[e]):
                if op.dma_sem is None:
                    lasts.append(op)
                    break
        lasts += list(self.last_dma.values())
        for e in self.ENG:
            self.pending[e] = list(lasts)

    def emit(self, nc, block, sems):
        for e in self.ENG:
            c = 0
            for op in self.ops[e]:
                if op.dma_sem is None and op.marked:
                    c += 1
                op.semval = c
        sched = self

        def run(ename, eng):
            waited = {}
            for op in sched.ops[ename]:
                for d in op.deps:
                    if d.dma_sem is not None:
                        key, v = d.dma_sem, d.dma_val
                    else:
                        key, v = d.eng, d.semval
                    if waited.get(key, 0) < v:
                        eng.wait_ge(sems[key], v)
                        waited[key] = v
                ins = op.fn(eng)
                if op.dma_sem is not None:
                    ins.then_inc(sems[op.dma_sem], 16)
                elif op.marked:
                    ins.then_inc(sems[ename], 1)

        @block.tensor
        def _(e):
            run("pe", e)

        @block.scalar
        def _(e):
            run("act", e)

        @block.vector
        def _(e):
            run("dve", e)

        @block.gpsimd
        def _(e):
            run("pool", e)

        @block.sync
        def _(e):
            run("sp", e)


def _nat_row_info(qr):
    rs = min(max(qr - 4, 0), 24)
    mstart = rs // 2
    nslots = 5 if (rs % 2) else 4
    if 4 <= qr <= 28:
        cls = qr % 2
    elif qr < 4:
        cls = 2 + qr
    else:
        cls = 6 + (qr - 29)
    return cls, rs, mstart, nslots


_NAT_CLS_REP = [4, 5, 0, 1, 2, 3, 29, 30, 31]


def build_nc(nseq, debug=None):
    nc = bass.Bass("TRN2", target_bir_lowering=False)
    SC = Sched()

    def din(name, shape, dt=F32):
        return nc.dram_tensor(name, list(shape), dt, kind="ExternalInput").ap()

    x_d = din("x", [nseq, S, D])
    y_d = nc.dram_tensor("y", [nseq, S, D], F32, kind="ExternalOutput").ap()
    wmix_d = din("wmix", [2, D, NMIX])
    wbr_d = din("wbr", [2, 8, 1024, 128])
    wout_d = din("wout", [2, D, D])
    wfi_d = din("wfi", [2, 11, D, 512])
    wfo_d = din("wfo", [2, FFN, D])
    cw_d = din("cw", [2, 4, 128, 512])
    norms_d = din("norms", [5, D])
    lamv_d = din("lamv", [2, 128])
    pvec_d = din("pvec", [128, 2 * PV_L + 6])
    rope_d = din("rope", [4, 128, S])
    nb_d = din("nbias", [2, 4, 128, 9 * 320])
    cst_d = din("cst", [3, 128, 128])
    dbg_d = {}
    if debug:
        for nm, shp in debug.items():
            dbg_d[nm] = nc.dram_tensor("dbg_" + nm, list(shp), F32, kind="ExternalOutput").ap()

    es = ExitStack()

    def sb(name, shape, dt):
        return es.enter_context(nc.sbuf_tensor(name, list(shape), dt))

    X = sb("X", [128, NT * D], F32)
    HT = sb("HT", [128, 8 * S], BF16)
    R = sb("R", [128, 18432], F32)
    W = sb("W", [128, 2 * 4096], BF16)
    PT = sb("PT", [128, 6 * 512], BF16)
    TMP = sb("TMP", [128, 3 * 512], F32)
    GB = sb("GB", [128, D], F32)
    HB = sb("HB", [128, 2 * D], BF16)
    PV = sb("PV", [128, 2 * PV_L + 6], F32)
    ST = sb("ST", [128, 64], F32)
    LQ = sb("LQ", [128, 128], F32)
    CST = sb("CST", [128, 3 * 128], BF16)
    PSP = [es.enter_context(nc.psum_tensor("psp%d" % i, [128, 1024], F32)) for i in range(4)]
    PS = [PSP[i // 2][:, (i % 2) * 512:(i % 2) * 512 + 512] for i in range(8)]

    dma_sems = ["w0", "w1", "x0", "x1", "x2", "x3", "tblc", "tbls", "gb", "cst", "pv", "out0", "out1", "lq", "dbg", "wb0", "wb1", "nb1"]
    sems = {}
    for nm in list(Sched.ENG[:4]) + dma_sems:
        sems[nm] = es.enter_context(nc.semaphore("s_" + nm))

    X3 = X[:, :].rearrange("p (t d) -> p t d", d=D)
    HT3 = HT[:, :].rearrange("p (k s) -> p k s", s=S)
    ident = CST[:, 0:128]
    ones_b = CST[:, 128:256]
    bones = CST[:, 256:384]

    OT_OFF, S_OFF, TBL_OFF = 0, 32768, 61440

    def rview(off, nel, dt):
        sz = 2 if dt == BF16 else 4
        a = R[:, off // 4:(off + nel * sz) // 4]
        return a.bitcast(dt) if dt != F32 else a

    def rkeys(off, nbytes):
        return [("R", i) for i in range(off // 1024, (off + nbytes + 1023) // 1024)]

    def slab(off):
        return rview(off, S, BF16)

    def slab_keys(off, c=None):
        if c is None:
            return rkeys(off, 4096)
        return rkeys(off + c * 1024, 1024)

    OTs = [OT_OFF + 4096 * i for i in range(8)]
    MT_OFF = S_OFF

    def HBK(i):
        return [("QZ", 0, 2 * i), ("QZ", 0, 2 * i + 1)]

    GBK = [("QZ", 1, b_) for b_ in range(4)]

    def GBWK(g_):
        return [("QZ", 1, 2 * g_), ("QZ", 1, 2 * g_ + 1)]

    qz_ctr = [0]

    def make_qz(Qs, qoff, c, nblk):
        slot = qz_ctr[0] % 2
        qz_ctr[0] += 1
        base = HB[:, 0:2048] if slot == 0 else GB[:, :].bitcast(BF16)
        qz = base.rearrange("p (b c) -> p b c", c=512)
        mcol0 = 2 * PV_L + (0 if nblk == 4 else 4)
        keys = []
        for b_ in range(nblk):
            k = ("QZ", slot, b_)
            keys.append(k)
            ts(qz[:, b_, :], Qs[:, c * 512:(c + 1) * 512], PV[:, mcol0 + b_:mcol0 + b_ + 1], None, ALU.mult, None,
               slab_keys(qoff, c) + [("PV",)], [k])
        return qz, keys

    bank_ctr = [0]

    live_acc = set()
    cooling = []

    def nb():
        while True:
            b = bank_ctr[0] % 8
            bank_ctr[0] += 1
            if b not in live_acc and b not in cooling:
                return b

    def release_acc(b):
        live_acc.discard(b)
        cooling.append(b)
        if len(cooling) > 2:
            cooling.pop(0)

    pt_ctr = [0]

    def npt():
        i = pt_ctr[0] % 6
        pt_ctr[0] += 1
        return i

    def PTv(i, n=512):
        return PT[:, i * 512:i * 512 + n]

    def TMPv(i, n=512):
        return TMP[:, i * 512:i * 512 + n]

    ev_ctr = [0]

    def mm(out, lhsT, rhs, start, stop, reads, writes, tp=None):
        if tp is None:
            SC.add("pe", lambda e: e.matmul(out, lhsT, rhs, start=start, stop=stop), reads, writes)
        else:
            SC.add("pe", lambda e: e.matmul(out, lhsT, rhs, start=start, stop=stop, tile_position=tp), reads, writes)

    def act(out, in_, func, reads, writes, **kw):
        SC.add("act", lambda e: e.activation(out=out, in_=in_, func=func, **kw), reads, writes)

    def copy_any(out, in_, reads, writes):
        ev_ctr[0] += 1
        if ev_ctr[0] % 2:
            SC.add("act", lambda e: e.activation(out=out, in_=in_, func=AF.Copy), reads, writes)
        else:
            SC.add("dve", lambda e: e.tensor_copy(out=out, in_=in_), reads, writes)

    def tt(out, in0, in1, op, reads, writes):
        SC.add("dve", lambda e: e.tensor_tensor(out=out, in0=in0, in1=in1, op=op), reads, writes)

    def stt(out, in0, scalar, in1, op0, op1, reads, writes):
        SC.add("dve", lambda e: e.scalar_tensor_tensor(out=out, in0=in0, scalar=scalar, in1=in1, op0=op0, op1=op1),
               reads, writes)

    def ts(out, in0, s1, s2, op0, op1, reads, writes):
        if s2 is None:
            SC.add("dve", lambda e: e.tensor_scalar(out=out, in0=in0, scalar1=s1, scalar2=None, op0=op0), reads, writes)
        else:
            SC.add("dve", lambda e: e.tensor_scalar(out=out, in0=in0, scalar1=s1, scalar2=s2, op0=op0, op1=op1),
                   reads, writes)

    def recip(out, in_, reads, writes):
        SC.add("dve", lambda e: e.reciprocal(out=out, in_=in_), reads, writes)

    def memset(ap, val, writes):
        SC.add("dve", lambda e: e.memset(ap, val), (), writes)

    def dma(q, out, in_, sem, reads, writes):
        SC.add(q, lambda e: e.dma_start(out=out, in_=in_), reads, writes, dma_sem=sem)

    def rstd_from(out, in_, n, reads, writes):
        SC.add("act", lambda e: e.activation(out=out, in_=in_, func=AF.Ln, scale=1.0 / n, bias=EPSC),
               list(reads) + [("EPS",)], writes)
        SC.add("act", lambda e: e.activation(out=out, in_=out, func=AF.Exp, scale=-0.5), writes, writes)

    def recip_act(out, in_, reads, writes):
        SC.add("act", lambda e: e.activation(out=out, in_=in_, func=AF.Ln), reads, writes)
        SC.add("act", lambda e: e.activation(out=out, in_=out, func=AF.Exp, scale=-1.0), writes, writes)

    EPSC = ST[:, 40:41]

    w_ctr = [0]

    def load_w(src, nk, ncols):
        slot = w_ctr[0] % 2
        w_ctr[0] += 1
        view = W[:, slot * 4096:slot * 4096 + nk * ncols].rearrange("p (k c) -> p k c", c=ncols)
        key = ("W", slot)
        dma("pool", view, src.rearrange("(k p) c -> p k c", p=128), "w%d" % slot, (), [key])
        return view, key

    def dbg_dump(name, ap, reads):
        if debug and name in dbg_d:
            dma("sp", dbg_d[name], ap, "dbg", reads, ())

    def load_consts():
        dma("pool", CST[:, :].rearrange("p (a c) -> p a c", c=128), cst_d.rearrange("a p c -> p a c"), "cst", (), [("CST",)])
        dma("sp", PV[:, :], pvec_d, "pv", (), [("PV",)])
        memset(EPSC, EPS, [("EPS",)])

    def rmsnorm_to_HT(norm_idx):
        dma("sp", GB[:, :], norms_d[norm_idx].partition_broadcast(128), "gb", (), GBK)
        for t in range(NT):
            s = t % 2
            hb = HB[:, s * D:(s + 1) * D]
            SC.add("act", lambda e, hb=hb, t=t: e.activation(out=hb, in_=X3[:, t, :], func=AF.Square,
                                                             accum_out=ST[:, t:t + 1]),
                   [("X", t)], HBK(s) + [("ss", t)])
        rstd_from(ST[:, 16:32], ST[:, 0:16], D, [("ss", t) for t in range(NT)], [("rs",)])
        for t in range(NT):
            s = t % 2
            hb = HB[:, s * D:(s + 1) * D]
            stt(hb, X3[:, t, :], ST[:, 16 + t:17 + t], GB[:, :], ALU.mult, ALU.mult,
                [("X", t), ("rs",)] + GBK, HBK(s))
            b = nb()
            psb = PS[b].bitcast(BF16)
            for kc in range(8):
                SC.add("pe", lambda e, psb=psb, hb=hb, kc=kc: e.transpose(psb[:, kc * 128:(kc + 1) * 128],
                                                                          hb[:, kc * 128:(kc + 1) * 128], ident),
                       HBK(s) + [("CST",)], [("ps", b)])
            c = t // 4
            copy_any(HT3[:, :, t * 128:(t + 1) * 128], psb.rearrange("p (k c) -> p k c", c=128),
                     [("ps", b)], [("HT", kc, c) for kc in range(8)])

    def proj_fm(wv, wkey, col0, M, c, bank):
        for kc in range(8):
            mm(PS[bank][0:M, :], wv[:, kc, col0:col0 + M], HT3[:, kc, c * 512:(c + 1) * 512],
               kc == 0, kc == 7, [wkey, ("HT", kc, c)], [("ps", bank)])

    def load_tbl(which, c):
        cosv = rview(TBL_OFF, 512, F32)
        sinv = rview(TBL_OFF + 2048, 512, F32)
        dma("sp", cosv, rope_d[2 * which, :, c * 512:(c + 1) * 512], "tblc", (), rkeys(TBL_OFF, 2048))
        dma("sp", sinv, rope_d[2 * which + 1, :, c * 512:(c + 1) * 512], "tbls", (), rkeys(TBL_OFF + 2048, 2048))
        return cosv, sinv, rkeys(TBL_OFF, 4096)

    def rope_out(dst, dkeys, by, byp, rows, tbl, g=None, gp=None, rs=None):
        cosv, sinv, tk = tbl
        r0, r1 = rows
        t1 = TMPv(0)[r0:r1, :]
        t2 = TMPv(1)[r0:r1, :]
        if g is None:
            tt(t1, PS[by][r0:r1, :], cosv[r0:r1, :], ALU.mult, [("ps", by)] + tk, [("TMP", 0)])
            tt(t2, PS[byp][r0:r1, :], sinv[r0:r1, :], ALU.mult, [("ps", byp)] + tk, [("TMP", 1)])
        else:
            stt(t1, PS[by][r0:r1, :], g[r0:r1, :], cosv[r0:r1, :], ALU.mult, ALU.mult, [("ps", by), ("PV",)] + tk, [("TMP", 0)])
            stt(t2, PS[byp][r0:r1, :], gp[r0:r1, :], sinv[r0:r1, :], ALU.mult, ALU.mult, [("ps", byp), ("PV",)] + tk, [("TMP", 1)])
        if rs is None:
            tt(dst, t1, t2, ALU.add, [("TMP", 0), ("TMP", 1)], dkeys)
        else:
            tt(t1, t1, t2, ALU.add, [("TMP", 0), ("TMP", 1)], [("TMP", 0)])
            tt(dst, t1, rs[r0:r1, :], ALU.mult, [("TMP", 0), ("TMP", 2)], dkeys)

    def v_token_major(wv, wkey, ncol, dst3, dst_off, groups):
        for t in range(NT):
            b = nb()
            for kc in range(8):
                mm(PS[b][:, 0:ncol], HT3[:, kc, t * 128:(t + 1) * 128], wv[:, kc, 0:ncol], kc == 0, kc == 7,
                   [wkey, ("HT", kc, t // 4)], [("ps", b)])
            for (s0, n, d0, dstride) in groups:
                src = PS[b][:, s0:s0 + 64 * n].rearrange("p (a c) -> p a c", c=64)
                width = dst3.shape[2]
                dst = dst3[:, t, d0:d0 + dstride * n].rearrange("p (a c) -> p a c", c=dstride)[:, :, 0:64] \
                    if dstride * n + d0 <= width else None
                if dst is None:
                    for a in range(n):
                        copy_any(dst3[:, t, d0 + a * dstride:d0 + a * dstride + 64], PS[b][:, s0 + 64 * a:s0 + 64 * a + 64],
                                 [("ps", b)], rkeys(dst_off + t * width * 2, width * 2))
                else:
                    copy_any(dst, src, [("ps", b)], rkeys(dst_off + t * width * 2, width * 2))

    acc_ctr = [0]
    pair_ctr = [0]
    pt2_ctr = [0]

    def attn_core(q_ap, qkeys, k_of, kkeys_of, v_of, vkeys_of, scale, tp, la=2):
        ob = 6 + (acc_ctr[0] % 2)
        acc_ctr[0] += 1
        live_acc.add(ob)
        slots = {}
        NP = NT // 2

        def score(p):
            pr = pair_ctr[0] % 3
            pair_ctr[0] += 1
            for hf in range(2):
                kt = 2 * p + hf
                mm(PS[2 * pr + hf][:, :], k_of(kt), q_ap, True, True, qkeys + kkeys_of(kt), [("ps", 2 * pr + hf)], tp=tp)
            sl = pt2_ctr[0] % 3
            pt2_ctr[0] += 1
            act(PT[:, sl * 1024:(sl + 1) * 1024], PSP[pr][:, :], AF.Exp, [("ps", 2 * pr), ("ps", 2 * pr + 1)],
                [("PT", 2 * sl), ("PT", 2 * sl + 1)], scale=scale)
            slots[p] = sl

        for i in range(NP + la):
            if i < NP:
                score(i)
            p = i - la
            if p >= 0:
                sl = slots[p]
                for hf in range(2):
                    kt = 2 * p + hf
                    mm(PS[ob][:, :], v_of(kt), PT[:, sl * 1024 + hf * 512:sl * 1024 + hf * 512 + 512], kt == 0, kt == NT - 1,
                       [("PT", 2 * sl + hf)] + vkeys_of(kt), [("ps", ob)])
        return ob

    def norm_rows(hpar):
        return ((0, 64), (64, 128)) if hpar == 0 else ((64, 128), (0, 64))

    def mixer_A(l, lam_init):
        Qo = [S_OFF, S_OFF + 4096]
        Ko = [S_OFF + 8192, S_OFF + 12288]
        Vo = S_OFF + 16384
        V3 = rview(Vo, NT * 384, BF16).rearrange("p (t c) -> p t c", c=384)
        memset(V3[:, :, 64:128], 1.0, rkeys(Vo, 12288))
        memset(V3[:, :, 256:320], 1.0, rkeys(Vo, 12288))
        dma("sp", LQ[:, :], lamv_d[l].partition_broadcast(128), "lq", (), [("LQ",)])
        tt(LQ[:, 0:32], LQ[:, 0:32], LQ[:, 32:64], ALU.mult, [("LQ",)], [("LQ",)])
        tt(LQ[:, 64:96], LQ[:, 64:96], LQ[:, 96:128], ALU.mult, [("LQ",)], [("LQ",)])
        SC.add("dve", lambda e: e.reduce_sum(out=ST[:, 32:33], in_=LQ[:, 0:32], axis=AX.X), [("LQ",)], [("lam",)])
        SC.add("dve", lambda e: e.reduce_sum(out=ST[:, 33:34], in_=LQ[:, 64:96], axis=AX.X), [("LQ",)], [("lam",)])
        act(ST[:, 34:36], ST[:, 32:34], AF.Exp, [("lam",)], [("lam",)])
        tt(ST[:, 36:37], ST[:, 35:36], ST[:, 34:35], ALU.subtract, [("lam",)], [("lam",)])
        ts(ST[:, 36:37], ST[:, 36:37], -lam_init, None, ALU.add, None, [("lam",)], [("lam",)])
        neglam = ST[:, 36:37]
        ts(ST[:, 37:38], PV[:, l * PV_L + PV_SUB:l * PV_L + PV_SUB + 1], 1.0 - lam_init, None, ALU.mult, None,
           [("PV",)], [("subg",)])
        subg = ST[:, 37:38]

        for (dsts, col0) in ((Qo, 0), (Ko, 512)):
            wv, wk = load_w(wmix_d[l][:, A0 + col0:A0 + col0 + 512], 8, 512)
            for c in range(4):
                tbl = load_tbl(0, c)
                for j in range(2):
                    b1, b2 = nb(), nb()
                    proj_fm(wv, wk, j * 128, 128, c, b1)
                    proj_fm(wv, wk, 256 + j * 128, 128, c, b2)
                    rope_out(slab(dsts[j])[:, c * 512:(c + 1) * 512], slab_keys(dsts[j], c), b1, b2, (0, 128), tbl)
        wv, wk = load_w(wmix_d[l][:, A0 + 1024:A0 + 1280], 8, 256)
        v_token_major(wv, wk, 256, V3, Vo, [(0, 2, 0, 128), (128, 2, 192, 128)])
        voffs = [0, 64, 192, 256]

        jc = [(j_, c_) for j_ in range(2) for c_ in range(4)]
        pending_subln = [None]
        qz_next = make_qz(slab(Qo[0]), Qo[0], 0, 4)
        for ji, (j, c) in enumerate(jc):
            Qs, Ks = slab(Qo[j]), slab(Ko[j])
            if True:
                qz, qzk = qz_next
                for hh in range(2):
                    h = 2 * j + hh
                    orow, drow = norm_rows(hh)
                    for comp in range(2):
                        blk = hh * 2 + comp
                        if blk == 1 and ji + 1 < len(jc):
                            jn, cn = jc[ji + 1]
                            qz_next = make_qz(slab(Qo[jn]), Qo[jn], cn, 4)
                        ob = attn_core(
                            qz[:, blk, :], [qzk[blk]],
                            lambda kt, Ks=Ks: Ks[:, kt * 128:(kt + 1) * 128],
                            lambda kt, j=j: slab_keys(Ko[j], kt // 4),
                            lambda kt, h=h: V3[:, kt, voffs[h]:voffs[h] + 128],
                            lambda kt: rkeys(Vo + kt * 768, 768),
                            32 ** -0.5, None)
                        if blk == 0 and pending_subln[0] is not None:
                            pending_subln[0]()
                            pending_subln[0] = None
                        tcomp = TMPv(comp)
                        recip(tcomp[orow[0]:orow[1], :], PS[ob][drow[0]:drow[1], :], [("ps", ob)], [("TMP", comp)])
                        tt(tcomp[orow[0]:orow[1], :], tcomp[orow[0]:orow[1], :], PS[ob][orow[0]:orow[1], :], ALU.mult,
                           [("ps", ob), ("TMP", comp)], [("TMP", comp)])
                        release_acc(ob)
                    stt(TMPv(0)[orow[0]:orow[1], :], TMPv(1)[orow[0]:orow[1], :], neglam[orow[0]:orow[1], :],
                        TMPv(0)[orow[0]:orow[1], :], ALU.mult, ALU.add, [("TMP", 0), ("TMP", 1), ("lam",)], [("TMP", 0)])
                def subln(j=j, c=c):
                    pi = npt()
                    act(PTv(pi), TMPv(0), AF.Square, [("TMP", 0)], [("PT", pi)])
                    b = nb()
                    mm(PS[b][:, :], bones, PTv(pi), True, True, [("PT", pi), ("CST",)], [("ps", b)])
                    rstd_from(TMPv(2), PS[b][:, :], 64, [("ps", b)], [("TMP", 2)])
                    stt(slab(OTs[0 + j])[:, c * 512:(c + 1) * 512], TMPv(0), subg, TMPv(2), ALU.mult, ALU.mult,
                        [("TMP", 0), ("TMP", 2), ("subg",)], slab_keys(OTs[0 + j], c))
                pending_subln[0] = subln
        pending_subln[0]()

    def mixer_B(l):
        Qo = [S_OFF, S_OFF + 4096]
        Ko = [S_OFF + 8192, S_OFF + 12288]
        Vo = S_OFF + 16384
        V3 = rview(Vo, NT * 384, BF16).rearrange("p (t c) -> p t c", c=384)
        memset(V3[:, :, 64:128], 1.0, rkeys(Vo, 12288))
        memset(V3[:, :, 256:320], 1.0, rkeys(Vo, 12288))
        wv, wk = load_w(wmix_d[l][:, B0:B0 + 512], 8, 512)
        for c in range(4):
            for j in range(4):
                b = nb()
                proj_fm(wv, wk, j * 128, 128, c, b)
                off = (Qo + Ko)[j]
                copy_any(slab(off)[:, c * 512:(c + 1) * 512], PS[b][:, :], [("ps", b)], slab_keys(off, c))
        wv, wk = load_w(wmix_d[l][:, B0 + 512:B0 + 768], 8, 256)
        v_token_major(wv, wk, 256, V3, Vo, [(0, 2, 0, 128), (128, 2, 192, 128)])
        voffs = [0, 64, 192, 256]
        for h in range(4):
            j, hh = h // 2, h % 2
            base = 64 * hh
            orow, drow = norm_rows(hh)
            Qs, Ks = slab(Qo[j]), slab(Ko[j])
            boff = TBL_OFF if h % 2 == 0 else OTs[4]
            tblv = rview(boff, 9 * 320, F32)
            tkeys = rkeys(boff, 9 * 320 * 4)
            dma("sp", tblv, nb_d[l, h], "tblc" if h % 2 == 0 else "nb1", (), tkeys)
            LA = 3
            info = {}
            obs = {}

            def stage1(qr, Qs=Qs, Ks=Ks, base=base, j=j, tblv=tblv, tkeys=tkeys):
                cls, rs, mstart, ns = _nat_row_info(qr)
                sbk = nb()
                for s_ in range(ns):
                    m = mstart + s_
                    mm(PS[sbk][:, s_ * 64:(s_ + 1) * 64], Ks[base:base + 64, m * 128:(m + 1) * 128],
                       Qs[base:base + 64, qr * 64:(qr + 1) * 64], True, True,
                       slab_keys(Qo[j], qr // 8) + slab_keys(Ko[j], m // 4), [("ps", sbk)])
                n = ns * 64
                stt(PS[sbk][:, 0:n], PS[sbk][:, 0:n], 0.125, tblv[:, cls * 320:cls * 320 + n], ALU.mult, ALU.add,
                    [("ps", sbk)] + tkeys, [("ps", sbk)])
                pi = npt()
                act(PTv(pi)[:, 0:n], PS[sbk][:, 0:n], AF.Exp, [("ps", sbk)], [("PT", pi)])
                info[qr] = (pi, mstart, ns)

            def stage3(qr, h=h, j=j, orow=orow, drow=drow):
                qc, q8 = qr // 8, qr % 8
                if q8 == 0:
                    obs[qc] = nb()
                    live_acc.add(obs[qc])
                ob = obs[qc]
                pi, mstart, ns = info[qr]
                for s_ in range(ns):
                    m = mstart + s_
                    mm(PS[ob][:, q8 * 64:(q8 + 1) * 64], V3[:, m, voffs[h]:voffs[h] + 128], PTv(pi)[:, s_ * 64:(s_ + 1) * 64],
                       s_ == 0, s_ == ns - 1, [("PT", pi)] + rkeys(Vo + m * 768, 768), [("ps", ob)])
                if q8 == 7:
                    t0 = TMPv(1)
                    recip_act(t0[orow[0]:orow[1], :], PS[ob][drow[0]:drow[1], :], [("ps", ob)], [("TMP", 1)])
                    tt(slab(OTs[2 + j])[orow[0]:orow[1], qc * 512:(qc + 1) * 512], t0[orow[0]:orow[1], :],
                       PS[ob][orow[0]:orow[1], :], ALU.mult, [("ps", ob), ("TMP", 1)], slab_keys(OTs[2 + j], qc))
                    release_acc(ob)

            for i in range(32 + LA):
                if i < 32:
                    stage1(i)
                if i >= LA:
                    stage3(i - LA)

    def mixer_C(l):
        CQ0, CQ1, CKV, KR, QH, KH, VH = [S_OFF + 4096 * i for i in range(7)]
        pv = l * PV_L
        wv, wk = load_w(wmix_d[l][:, C0:C0 + 512], 8, 512)
        for c in range(4):
            cs = slice(c * 512, (c + 1) * 512)
            b0, b1, b2 = nb(), nb(), nb()
            proj_fm(wv, wk, 0, 128, c, b0)
            proj_fm(wv, wk, 128, 64, c, b1)
            proj_fm(wv, wk, 192, 128, c, b2)
            p0, p1 = npt(), npt()
            act(PTv(p0), PS[b0][:, :], AF.Square, [("ps", b0)], [("PT", p0)])
            act(PTv(p1)[0:64, :], PS[b1][0:64, :], AF.Square, [("ps", b1)], [("PT", p1)])
            b3 = nb()
            mm(PS[b3][:, :], ones_b, PTv(p0), True, False, [("PT", p0), ("CST",)], [("ps", b3)])
            mm(PS[b3][:, :], ones_b[0:64, :], PTv(p1)[0:64, :], False, True, [("PT", p1), ("CST",)], [("ps", b3)])
            rstd_from(TMPv(2), PS[b3][:, :], 192, [("ps", b3)], [("TMP", 2)])
            stt(slab(CQ0)[:, cs], PS[b0][:, :], PV[:, pv + PV_CQ0:pv + PV_CQ0 + 1], TMPv(2), ALU.mult, ALU.mult,
                [("ps", b0), ("TMP", 2), ("PV",)], slab_keys(CQ0, c))
            stt(slab(CQ1)[0:64, cs], PS[b1][0:64, :], PV[0:64, pv + PV_CQ1:pv + PV_CQ1 + 1], TMPv(2)[0:64, :], ALU.mult, ALU.mult,
                [("ps", b1), ("TMP", 2), ("PV",)], slab_keys(CQ1, c))
            p2 = npt()
            act(PTv(p2), PS[b2][:, :], AF.Square, [("ps", b2)], [("PT", p2)])
            b4 = nb()
            mm(PS[b4][:, :], ones_b, PTv(p2), True, True, [("PT", p2), ("CST",)], [("ps", b4)])
            rstd_from(TMPv(2), PS[b4][:, :], 128, [("ps", b4)], [("TMP", 2)])
            stt(slab(CKV)[:, cs], PS[b2][:, :], PV[:, pv + PV_CKV:pv + PV_CKV + 1], TMPv(2), ALU.mult, ALU.mult,
                [("ps", b2), ("TMP", 2), ("PV",)], slab_keys(CKV, c))
            tbl = load_tbl(0, c)
            b5, b6 = nb(), nb()
            proj_fm(wv, wk, 320, 96, c, b5)
            proj_fm(wv, wk, 416, 96, c, b6)
            rope_out(slab(KR)[64:96, cs], slab_keys(KR, c), b5, b6, (64, 96), tbl)
        for h in range(4):
            hh = h % 2
            orow, drow = norm_rows(hh)
            wslot = w_ctr[0] % 2
            w_ctr[0] += 1
            wc = W[:, wslot * 4096:wslot * 4096 + 512]
            wk = ("W", wslot)
            dma("pool", wc, cw_d[l, h], "w%d" % wslot, (), [wk])
            VH3 = rview(VH, NT * 128, BF16).rearrange("p (t c) -> p t c", c=128)
            vo = 0 if hh == 0 else 64
            memset(VH3[:, :, 64 - vo:128 - vo], 1.0, rkeys(VH, 4096))
            for half in range(2):
                b = nb()
                for i in range(8):
                    t = half * 8 + i
                    mm(PS[b][:, i * 64:(i + 1) * 64], slab(CKV)[:, t * 128:(t + 1) * 128], wc[:, 448:512], True, True,
                       [wk] + slab_keys(CKV, t // 4), [("ps", b)])
                copy_any(VH3[:, half * 8:(half + 1) * 8, vo:vo + 64], PS[b][:, :].rearrange("p (a c) -> p a c", c=64),
                         [("ps", b)], rkeys(VH, 4096))
            for c in range(4):
                cs = slice(c * 512, (c + 1) * 512)
                b2 = nb()
                mm(PS[b2][0:64, :], wc[:, 384:448], slab(CKV)[:, cs], True, True, [wk] + slab_keys(CKV, c), [("ps", b2)])
                copy_any(slab(KH)[0:64, cs], PS[b2][0:64, :], [("ps", b2)], slab_keys(KH, c))
                copy_any(slab(KH)[64:96, cs], slab(KR)[64:96, cs], slab_keys(KR, c), slab_keys(KH, c))

            def q_prologue(c, wc=wc, wk=wk):
                cs = slice(c * 512, (c + 1) * 512)
                tbl = load_tbl(0, c)
                b0, b1 = nb(), nb()
                mm(PS[b0][0:96, :], wc[:, 0:96], slab(CQ0)[:, cs], True, False, [wk] + slab_keys(CQ0, c), [("ps", b0)])
                mm(PS[b0][0:96, :], wc[0:64, 192:288], slab(CQ1)[0:64, cs], False, True, [wk] + slab_keys(CQ1, c), [("ps", b0)])
                mm(PS[b1][0:96, :], wc[:, 96:192], slab(CQ0)[:, cs], True, False, [wk] + slab_keys(CQ0, c), [("ps", b1)])
                mm(PS[b1][0:96, :], wc[0:64, 288:384], slab(CQ1)[0:64, cs], False, True, [wk] + slab_keys(CQ1, c), [("ps", b1)])
                copy_any(slab(QH)[0:64, cs], PS[b0][0:64, :], [("ps", b0)], slab_keys(QH, c))
                rope_out(slab(QH)[64:96, cs], slab_keys(QH, c), b0, b1, (64, 96), tbl)

            q_prologue(0)
            for c in range(4):
                if c + 1 < 4:
                    q_prologue(c + 1)
                ob = attn_core(
                    slab(QH)[0:96, c * 512:(c + 1) * 512], slab_keys(QH, c),
                    lambda kt: slab(KH)[0:96, kt * 128:(kt + 1) * 128],
                    lambda kt: slab_keys(KH, kt // 4),
                    lambda kt: VH3[:, kt, :],
                    lambda kt: rkeys(VH, 4096),
                    96 ** -0.5, None)
                t0 = TMPv(2)
                recip_act(t0[orow[0]:orow[1], :], PS[ob][drow[0]:drow[1], :], [("ps", ob)], [("TMP", 2)])
                tt(slab(OTs[4 + h // 2])[orow[0]:orow[1], c * 512:(c + 1) * 512], t0[orow[0]:orow[1], :],
                   PS[ob][orow[0]:orow[1], :], ALU.mult, [("ps", ob), ("TMP", 2)], slab_keys(OTs[4 + h // 2], c))
                release_acc(ob)

    def mixer_D(l):
        Qo = [S_OFF, S_OFF + 4096]
        Ko = [S_OFF + 8192, S_OFF + 12288]
        Vo = S_OFF + 16384
        V3 = rview(Vo, NT * 384, BF16).rearrange("p (t c) -> p t c", c=384)
        memset(V3[:, :, 64:128], 1.0, rkeys(Vo, 12288))
        memset(V3[:, :, 256:320], 1.0, rkeys(Vo, 12288))
        pv = l * PV_L
        for (dsts, col0, gc, gpc) in ((Qo, 0, PV_DQ, PV_DQP), (Ko, 512, PV_DK, PV_DKP)):
            wv, wk = load_w(wmix_d[l][:, D0 + col0:D0 + col0 + 512], 8, 512)
            for c in range(4):
                tbl = load_tbl(1, c)
                for j in range(2):
                    b1, b2 = nb(), nb()
                    proj_fm(wv, wk, j * 128, 128, c, b1)
                    proj_fm(wv, wk, 256 + j * 128, 128, c, b2)
                    pi = npt()
                    act(PTv(pi), PS[b1][:, :], AF.Square, [("ps", b1)], [("PT", pi)])
                    b3 = nb()
                    mm(PS[b3][:, :], bones, PTv(pi), True, True, [("PT", pi), ("CST",)], [("ps", b3)])
                    rstd_from(TMPv(2), PS[b3][:, :], 64, [("ps", b3)], [("TMP", 2)])
                    rope_out(slab(dsts[j])[:, c * 512:(c + 1) * 512], slab_keys(dsts[j], c), b1, b2, (0, 128), tbl,
                             g=PV[:, pv + gc:pv + gc + 1], gp=PV[:, pv + gpc:pv + gpc + 1], rs=TMPv(2))
        wv, wk = load_w(wmix_d[l][:, D0 + 1024:D0 + 1152], 8, 128)
        v_token_major(wv, wk, 128, V3, Vo, [(0, 1, 0, 128), (0, 1, 128, 128), (64, 1, 192, 128), (64, 1, 320, 128)])
        voffs = [0, 64, 192, 256]
        jc = [(j_, c_) for j_ in range(2) for c_ in range(4)]
        qz_next = make_qz(slab(Qo[0]), Qo[0], 0, 2)
        for ji, (j, c) in enumerate(jc):
            Qs, Ks = slab(Qo[j]), slab(Ko[j])
            if True:
                qz, qzk = qz_next
                for hh in range(2):
                    h = 2 * j + hh
                    orow, drow = norm_rows(hh)
                    if hh == 1 and ji + 1 < len(jc):
                        jn, cn = jc[ji + 1]
                        qz_next = make_qz(slab(Qo[jn]), Qo[jn], cn, 2)
                    ob = attn_core(
                        qz[:, hh, :], [qzk[hh]],
                        lambda kt, Ks=Ks: Ks[:, kt * 128:(kt + 1) * 128],
                        lambda kt, j=j: slab_keys(Ko[j], kt // 4),
                        lambda kt, h=h: V3[:, kt, voffs[h]:voffs[h] + 128],
                        lambda kt: rkeys(Vo + kt * 768, 768),
                        0.125, None)
                    t0 = TMPv(1)
                    recip(t0[orow[0]:orow[1], :], PS[ob][drow[0]:drow[1], :], [("ps", ob)], [("TMP", 1)])
                    tt(slab(OTs[6 + j])[orow[0]:orow[1], c * 512:(c + 1) * 512], t0[orow[0]:orow[1], :],
                       PS[ob][orow[0]:orow[1], :], ALU.mult, [("ps", ob), ("TMP", 1)], slab_keys(OTs[6 + j], c))
                    release_acc(ob)

    def merge_and_out(l):
        MT3 = rview(MT_OFF, 8 * S, BF16).rearrange("p (k s) -> p k s", s=S)
        for oc in range(8):
            wg, wgk = load_w(wmix_d[l][:, G0 + oc * 512:G0 + (oc + 1) * 512], 8, 512)
            gslot = oc % 2
            wb = GB[:, gslot * 512:(gslot + 1) * 512].bitcast(BF16).rearrange("p (k c) -> p k c", c=128)
            wbk = GBWK(gslot)
            dma("pool", wb, wbr_d[l, oc].rearrange("(k p) c -> p k c", p=128), "wb%d" % gslot, (), wbk)
            for c in range(4):
                cs = slice(c * 512, (c + 1) * 512)
                for i in range(4):
                    bg, bb = nb(), nb()
                    proj_fm(wg, wgk, i * 128, 128, c, bg)
                    for kc in range(2):
                        mm(PS[bb][:, :], wb[:, 2 * i + kc, :], slab(OTs[2 * i + kc])[:, cs], kc == 0, kc == 1,
                           wbk + slab_keys(OTs[2 * i + kc], c), [("ps", bb)])
                    act(TMPv(1), PS[bg][:, :], AF.Sigmoid, [("ps", bg)], [("TMP", 1)])
                    if i == 0:
                        tt(TMPv(0), TMPv(1), PS[bb][:, :], ALU.mult, [("TMP", 1), ("ps", bb)], [("TMP", 0)])
                    else:
                        tt(TMPv(1), TMPv(1), PS[bb][:, :], ALU.mult, [("TMP", 1), ("ps", bb)], [("TMP", 1)])
                        if i < 3:
                            tt(TMPv(0), TMPv(0), TMPv(1), ALU.add, [("TMP", 0), ("TMP", 1)], [("TMP", 0)])
                        else:
                            tt(MT3[:, oc, cs], TMPv(0), TMPv(1), ALU.add, [("TMP", 0), ("TMP", 1)],
                               rkeys(MT_OFF + oc * 4096 + c * 1024, 1024))
        for cg in range(2):
            wv, wk = load_w(wout_d[l][:, cg * 512:(cg + 1) * 512], 8, 512)
            for t in range(NT):
                b = nb()
                for kc in range(8):
                    mm(PS[b][:, :], MT3[:, kc, t * 128:(t + 1) * 128], wv[:, kc, :], kc == 0, kc == 7,
                       [wk] + rkeys(MT_OFF + kc * 4096 + (t // 4) * 1024, 1024), [("ps", b)])
                tt(X3[:, t, cg * 512:(cg + 1) * 512], X3[:, t, cg * 512:(cg + 1) * 512], PS[b][:, :], ALU.add,
                   [("ps", b), ("X", t)], [("X", t)])

    def ffn(l):
        AT_OFF = 0
        GU = []
        for i_ in range(2):
            g_off = 32768 + i_ * 16448
            u_off = g_off + 8256
            GU.append((rview(g_off, 2050, F32), rview(u_off, 2048, F32), rkeys(g_off, 8200), rkeys(u_off, 8192)))
        pv = l * PV_L + PV_CONV
        tmp_ctr = [0]
        AT3 = rview(AT_OFF, 8 * S, BF16).rearrange("p (k s) -> p k s", s=S)
        groups = ((0, 8), (8, 8), (16, 6))

        def stage_b1(f, fc, c):
            Gv, Uv, gk, uk = GU[fc % 2]
            w0 = PV[:, pv + 4 * fc + 0:pv + 4 * fc + 1]
            w1 = PV[:, pv + 4 * fc + 1:pv + 4 * fc + 2]
            w2 = PV[:, pv + 4 * fc + 2:pv + 4 * fc + 3]
            bb = PV[:, pv + 4 * fc + 3:pv + 4 * fc + 4]
            c0 = c * 512
            ti = tmp_ctr[0] % 5
            tmp_ctr[0] += 1
            if ti < 3:
                T = TMPv(ti)
                tk = [("TMP", ti)]
            else:
                T = GB[:, (ti - 3) * 512:(ti - 2) * 512]
                tk = GBWK(ti - 3)
            SC.add("act", lambda e, T=T, Gv=Gv, c0=c0, w1=w1, bb=bb: e.activation(
                out=T, in_=Gv[:, c0 + 1:c0 + 513], func=AF.Identity, scale=w1, bias=bb), gk + [("PV",)], tk)
            stt(T, Gv[:, c0:c0 + 512], w0, T, ALU.mult, ALU.add, gk + [("PV",)] + tk, tk)
            stt(T, Gv[:, c0 + 2:c0 + 514], w2, T, ALU.mult, ALU.add, gk + [("PV",)] + tk, tk)
            return T, tk

        pend_mult = [None]
        pend_silu = [None]

        def flush_mult():
            if pend_mult[0] is not None:
                pend_mult[0]()
                pend_mult[0] = None

        def flush_silu():
            if pend_silu[0] is not None:
                pend_silu[0]()
                pend_silu[0] = None

        def flush_all():
            flush_mult()
            flush_silu()
            flush_mult()

        def stage_b2(f, fc, c, T, tk):
            Gv, Uv, gk, uk = GU[fc % 2]
            c0 = c * 512

            def mult():
                tt(AT3[:, f, c0:c0 + 512], T, Uv[:, c0:c0 + 512], ALU.mult, tk + uk,
                   rkeys(AT_OFF + f * 4096 + c * 1024, 1024))

            def silu():
                act(T, T, AF.Silu, tk, tk)
                pend_mult[0] = mult
            flush_silu()
            pend_silu[0] = silu

        def ffn_out(f0, nf):
            for cg in range(2):
                wv, wk = load_w(wfo_d[l][f0 * 128:(f0 + nf) * 128, cg * 512:(cg + 1) * 512], nf, 512)
                for t in range(NT):
                    b = nb()
                    for f in range(nf):
                        mm(PS[b][:, :], AT3[:, f, t * 128:(t + 1) * 128], wv[:, f, :], f == 0, f == nf - 1,
                           [wk] + rkeys(AT_OFF + f * 4096 + (t // 4) * 1024, 1024), [("ps", b)])
                    tt(X3[:, t, cg * 512:(cg + 1) * 512], X3[:, t, cg * 512:(cg + 1) * 512], PS[b][:, :], ALU.add,
                       [("ps", b), ("X", t)], [("X", t)])

        prev = None
        wv = wk = None
        for gi, (f0, nf) in enumerate(groups):
            for f in range(nf):
                fc = f0 + f
                if f % 2 == 0 or (f == 1 and gi > 0):
                    wv, wk = load_w(wfi_d[l, fc // 2], 8, 512)
                f2 = fc % 2
                Gv, Uv, gk, uk = GU[fc % 2]
                memset(Gv[:, 0:1], 0.0, gk)
                memset(Gv[:, 2049:2050], 0.0, gk)
                for c in range(4):
                    if prev is not None:
                        Tt = stage_b1(prev[0], prev[1], c)
                        flush_mult()
                    bu, bg = nb(), nb()
                    proj_fm(wv, wk, f2 * 256, 128, c, bu)
                    proj_fm(wv, wk, f2 * 256 + 128, 128, c, bg)
                    act(Uv[:, c * 512:(c + 1) * 512], PS[bu][:, :], AF.Copy, [("ps", bu)], uk)
                    act(Gv[:, 1 + c * 512:1 + (c + 1) * 512], PS[bg][:, :], AF.Copy, [("ps", bg)], gk)
                    if prev is not None:
                        stage_b2(prev[0], prev[1], c, *Tt)
                prev = None
                if f == 0 and gi > 0:
                    flush_all()
                    ffn_out(*groups[gi - 1])
                prev = (f, fc)
                if f == 0 and gi > 0:
                    pass
        for c in range(4):
            Tt = stage_b1(prev[0], prev[1], c)
            flush_mult()
            stage_b2(prev[0], prev[1], c, *Tt)
        flush_all()
        ffn_out(*groups[-1])

    def final_norm_store(si):
        dma("sp", GB[:, :], norms_d[4].partition_broadcast(128), "gb", (), GBK)
        for t in range(NT):
            s = t % 2
            hb = HB[:, s * D:(s + 1) * D]
            SC.add("act", lambda e, hb=hb, t=t: e.activation(out=hb, in_=X3[:, t, :], func=AF.Square,
                                                             accum_out=ST[:, t:t + 1]),
                   [("X", t)], HBK(s) + [("ss", t)])
        rstd_from(ST[:, 16:32], ST[:, 0:16], D, [("ss", t) for t in range(NT)], [("rs",)])
        for t in range(NT):
            s = t % 2
            ys = rview(s * 4096, D, F32)
            yk = rkeys(s * 4096, 4096)
            stt(ys, X3[:, t, :], ST[:, 16 + t:17 + t], GB[:, :], ALU.mult, ALU.mult, [("X", t), ("rs",)] + GBK, yk)
            dma("sp", y_d[si, t * 128:(t + 1) * 128, :], ys, "out%d" % s, yk, ())

    load_consts()
    for si in range(nseq):
        for i in range(4):
            dma("sp", X3[:, 4 * i:4 * i + 4, :], x_d[si, 512 * i:512 * (i + 1), :].rearrange("(t p) d -> p t d", p=128),
                "x%d" % i, (), [("X", t) for t in range(4 * i, 4 * i + 4)])
        for l in range(2):
            lam_init = 0.8 - 0.6 * math.exp(-0.3 * l)
            SC.phase = "norm1"
            rmsnorm_to_HT(l)
            SC.phase = "A"
            mixer_A(l, lam_init)
            SC.phase = "B"
            mixer_B(l)
            SC.phase = "C"
            mixer_C(l)
            SC.phase = "D"
            mixer_D(l)
            SC.phase = "merge"
            merge_and_out(l)
            SC.phase = "norm2"
            rmsnorm_to_HT(2 + l)
            SC.phase = "ffn"
            ffn(l)
        final_norm_store(si)
    SC.barrier()
    SC.add("sp", lambda e: e.nop(), (), ())

    nc._pe_phase = SC.pe_phase
    with nc.Block() as block:
        SC.emit(nc, block, sems)
    es.close()
    return nc


def _partner_perm(n):
    idx = np.arange(n)
    return (idx // 32) * 32 + ((idx % 32) + 16) % 32


def _prep_shared(inp):
    f32 = np.float32
    w_in = np.asarray(inp["w_in"], f32)
    L = 2
    wmix = np.zeros((L, D, NMIX), f32)
    aq, ak, av = 0, 256, 512
    bq, bk, bv = 768, 1024, 1280
    ccq, cckv, ckr = 1536, 1728, 1856
    dq, dk, dv = 1888, 2144, 2272
    g0 = 2400
    p256 = _partner_perm(256)
    p128 = _partner_perm(128)
    p32 = _partner_perm(32)
    for l in range(L):
        w = w_in[l]
        m = wmix[l]
        m[:, A0:A0 + 256] = w[:, aq:aq + 256]
        m[:, A0 + 256:A0 + 512] = w[:, aq + p256]
        m[:, A0 + 512:A0 + 768] = w[:, ak:ak + 256]
        m[:, A0 + 768:A0 + 1024] = w[:, ak + p256]
        m[:, A0 + 1024:A0 + 1280] = w[:, av:av + 256]
        m[:, B0:B0 + 768] = w[:, bq:bq + 768]
        m[:, C0:C0 + 320] = w[:, ccq:ccq + 320]
        m[:, C0 + 320:C0 + 384] = w[:, cckv:cckv + 64]
        m[:, C0 + 384:C0 + 416] = w[:, ckr:ckr + 32]
        m[:, C0 + 416:C0 + 480] = w[:, cckv:cckv + 64]
        m[:, C0 + 480:C0 + 512] = w[:, ckr + p32]
        m[:, D0:D0 + 256] = w[:, dq:dq + 256]
        m[:, D0 + 256:D0 + 512] = w[:, dq + p256]
        kd = np.concatenate([w[:, dk:dk + 64], w[:, dk:dk + 64], w[:, dk + 64:dk + 128], w[:, dk + 64:dk + 128]], axis=1)
        kp = w[:, dk + p128]
        kdp = np.concatenate([kp[:, 0:64], kp[:, 0:64], kp[:, 64:128], kp[:, 64:128]], axis=1)
        m[:, D0 + 512:D0 + 768] = kd
        m[:, D0 + 768:D0 + 1024] = kdp
        m[:, D0 + 1024:D0 + 1152] = w[:, dv:dv + 128]
        gates = w[:, g0:g0 + 4096].reshape(D, 4, 8, 128).transpose(0, 2, 1, 3).reshape(D, 4096)
        m[:, G0:G0 + 4096] = gates
    wbr = np.asarray(inp["w_branch"], f32)
    wbr2 = wbr.reshape(L, 4, 256, 8, 128).transpose(0, 3, 1, 2, 4).reshape(L, 8, 1024, 128)
    wfi = np.asarray(inp["w_ffn_in"], f32)
    u = wfi[:, :, :FFN].reshape(L, D, NFC, 128)
    g = wfi[:, :, FFN:].reshape(L, D, NFC, 128)
    ug = np.stack([u, g], axis=3)
    wfi2 = ug.reshape(L, D, 11, 512).transpose(0, 2, 1, 3)
    wuq = np.asarray(inp["c_w_uq"], f32)
    wukv = np.asarray(inp["c_w_ukv"], f32)
    cw = np.zeros((L, 4, 128, 512), f32)
    for l in range(L):
        for h in range(4):
            qh = wuq[l][:, h * 96:(h + 1) * 96]
            perm96 = np.arange(96)
            perm96[64:96] = 64 + p32
            qhp = qh[:, perm96]
            cw[l, h, :, 0:96] = qh[0:128]
            cw[l, h, :, 96:192] = qhp[0:128]
            cw[l, h, 0:64, 192:288] = qh[128:192]
            cw[l, h, 0:64, 288:384] = qhp[128:192]
            cw[l, h, :, 384:448] = wukv[l][:, h * 128:h * 128 + 64]
            cw[l, h, :, 448:512] = wukv[l][:, h * 128 + 64:h * 128 + 128]
    norms = np.stack([np.asarray(inp["attn_norm"], f32)[0], np.asarray(inp["attn_norm"], f32)[1],
                      np.asarray(inp["ffn_norm"], f32)[0], np.asarray(inp["ffn_norm"], f32)[1],
                      np.asarray(inp["final_norm"], f32)], axis=0)
    lamv = np.concatenate([np.asarray(inp[k], f32) for k in ("a_lambda_q1", "a_lambda_k1", "a_lambda_q2", "a_lambda_k2")], axis=1)
    pvec = np.zeros((128, 2 * PV_L + 6), f32)
    for b_ in range(4):
        pvec[32 * b_:32 * b_ + 32, 2 * PV_L + b_] = 1.0
    for b_ in range(2):
        pvec[64 * b_:64 * b_ + 64, 2 * PV_L + 4 + b_] = 1.0
    p64 = _partner_perm(64)
    for l in range(L):
        o = l * PV_L
        dqn = np.asarray(inp["d_q_norm"], f32)[l]
        dkn = np.asarray(inp["d_k_norm"], f32)[l]
        pvec[:, o + PV_DQ] = np.tile(dqn, 2)
        pvec[:, o + PV_DQP] = np.tile(dqn[p64], 2)
        pvec[:, o + PV_DK] = np.tile(dkn, 2)
        pvec[:, o + PV_DKP] = np.tile(dkn[p64], 2)
        pvec[:, o + PV_SUB] = np.tile(np.asarray(inp["a_subln"], f32)[l], 2)
        cqn = np.asarray(inp["c_q_norm"], f32)[l]
        pvec[:, o + PV_CQ0] = cqn[0:128]
        pvec[0:64, o + PV_CQ1] = cqn[128:192]
        pvec[:, o + PV_CKV] = np.asarray(inp["c_kv_norm"], f32)[l]
        cwt = np.asarray(inp["ffn_conv_w"], f32)[l].reshape(3, NFC, 128)
        cbt = np.asarray(inp["ffn_conv_b"], f32)[l].reshape(NFC, 128)
        for fc in range(NFC):
            for k in range(3):
                pvec[:, o + PV_CONV + 4 * fc + k] = cwt[k, fc]
            pvec[:, o + PV_CONV + 4 * fc + 3] = cbt[fc]
    t = np.arange(S, dtype=f32)
    inv = (np.float32(10000.0) ** (-np.arange(0, 32, 2, dtype=f32) / np.float32(32))).astype(f32)
    p = np.arange(128)
    fi = p % 16
    sgn = np.where((p % 32) < 16, -1.0, 1.0).astype(f32)
    angA = (t[None, :] * inv[fi][:, None]).astype(f32)
    pos_row = (np.arange(S) // 64).astype(f32)
    pos_col = (np.arange(S) % 64).astype(f32)
    posD = np.where(((p % 64) < 32)[:, None], pos_row[None, :], pos_col[None, :]).astype(f32)
    angD = (posD * inv[fi][:, None]).astype(f32)
    rope = np.stack([np.cos(angA), np.sin(angA) * sgn[:, None], np.cos(angD), np.sin(angD) * sgn[:, None]], axis=0).astype(f32)
    rpb = np.asarray(inp["b_rpb"], f32)
    nbias = np.full((L, 4, 128, 9, 5, 64), -30000.0, f32)
    kc = np.arange(64)[:, None]
    qc = np.arange(64)[None, :]
    cs = np.clip(qc - 8, 0, 48)
    colmask = (kc >= cs) & (kc < cs + 16)
    dci = np.clip(kc - qc + 15, 0, 30)
    for ci, qr in enumerate(_NAT_CLS_REP):
        cls, rs, mstart, ns = _nat_row_info(qr)
        assert cls == ci
        for s_ in range(ns):
            for b in range(2):
                kr = 2 * (mstart + s_) + b
                if kr < rs or kr >= rs + 8:
                    continue
                dr = kr - qr + 7
                vals = rpb[:, :, dr][:, :, dci]
                nbias[:, :, 64 * b:64 * b + 64, ci, s_, :] = np.where(colmask[None, None], vals, np.float32(-30000.0))
    nbias = nbias.reshape(L, 4, 128, 9 * 320)
    cst = np.zeros((3, 128, 128), f32)
    cst[0] = np.eye(128, dtype=f32)
    cst[1] = 1.0
    cst[2, 0:64, 0:64] = 1.0
    cst[2, 64:128, 64:128] = 1.0
    return {
        "wmix": wmix, "wbr": np.ascontiguousarray(wbr2), "wout": np.asarray(inp["w_out"], f32),
        "wfi": np.ascontiguousarray(wfi2), "wfo": np.asarray(inp["w_ffn_out"], f32), "cw": cw,
        "norms": np.ascontiguousarray(norms), "lamv": np.ascontiguousarray(lamv), "pvec": pvec,
        "rope": rope, "nbias": np.ascontiguousarray(nbias), "cst": cst,
    }


def kernel(**inputs):
    xp = np.asarray(inputs["x_prompt"], np.float32)
    xs = np.asarray(inputs["x_sample"], np.float32)
    shared = _prep_shared(inputs)
    nc = build_nc(SEQ_PER_CORE)
    in_maps = []
    for c in range(N_CORES):
        xc = np.concatenate([xp[4 * c:4 * c + 4], xs[c:c + 1]], axis=0)
        m = dict(shared)
        m["x"] = np.ascontiguousarray(xc)
        in_maps.append(m)
    res = run_bass_kernel_spmd(nc, in_maps, core_ids=list(range(N_CORES)))
    yp = np.empty_like(xp)
    ys = np.empty_like(xs)
    for c in range(N_CORES):
        y = res.results[c]["y"]
        yp[4 * c:4 * c + 4] = y[0:4]
        ys[c] = y[4]
    return (yp, ys)
```

```python
import math
from contextlib import ExitStack

import numpy as np
import concourse.bass as bass
import concourse.mybir as mybir
from concourse.bass_utils import run_bass_kernel_spmd

F32 = mybir.dt.float32
BF16 = mybir.dt.bfloat16
AF = mybir.ActivationFunctionType
ALU = mybir.AluOpType
AX = mybir.AxisListType

S = 2048
D = 1024
NT = 16
EPS = 1e-6
FFN = 2816
NFC = 22
N_CORES = 8
SEQ_PER_CORE = 5

A0 = 0
B0 = 1280
C0 = 2048
D0 = 2560
G0 = 3712
NMIX = G0 + 4096

PV_DQ, PV_DQP, PV_DK, PV_DKP, PV_SUB, PV_CQ0, PV_CQ1, PV_CKV, PV_CONV = 0, 1, 2, 3, 4, 5, 6, 7, 8
PV_L = 8 + 4 * NFC


class _Op:
    __slots__ = ("eng", "fn", "deps", "marked", "idx", "dma_sem", "dma_val", "semval", "rw")


class Sched:
    ENG = ("pe", "act", "dve", "pool", "sp")

    def __init__(self):
        self.ops = {e: [] for e in self.ENG}
        self.state = {}
        self.dma_count = {}
        self.pending = {e: [] for e in self.ENG}
        self.last_dma = {}
        self.phase = ""
        self.pe_phase = []

    def add(self, eng, fn, reads=(), writes=(), dma_sem=None):
        op = _Op()
        op.eng, op.fn, op.marked, op.dma_sem = eng, fn, False, dma_sem
        op.dma_val = 0
        op.semval = 0
        op.rw = (tuple(reads), tuple(writes))
        deps = {}

        def dep(d):
            if d is None:
                return
            if d.dma_sem is not None:
                k = ("dma", d.dma_sem)
                if k not in deps or deps[k].dma_val < d.dma_val:
                    deps[k] = d
            else:
                if d.eng == "pe" and eng == "pe" and dma_sem is None:
                    return
                k = d.eng
                if k not in deps or deps[k].idx < d.idx:
                    deps[k] = d

        st_ = self.state
        for k in reads:
            s = st_.get(k)
            if s is not None:
                dep(s[0])
        for k in writes:
            s = st_.get(k)
            if s is not None:
                dep(s[0])
                for r in s[1].values():
                    dep(r)
                for r in s[2]:
                    dep(r)
        for d in self.pending[eng]:
            dep(d)
        self.pending[eng] = []
        if dma_sem is not None:
            c = self.dma_count.get(dma_sem, 0) + 1
            self.dma_count[dma_sem] = c
            op.dma_val = 16 * c
            self.last_dma[dma_sem] = op
        op.idx = len(self.ops[eng])
        self.ops[eng].append(op)
        if eng == "pe":
            self.pe_phase.append(self.phase)
        op.deps = list(deps.values())
        for d in op.deps:
            if d.dma_sem is None:
                d.marked = True
        for k in writes:
            st_[k] = [op, {}, []]
        for k in reads:
            s = st_.get(k)
            if s is None:
                s = [None, {}, []]
                st_[k] = s
            if dma_sem is not None:
                s[2].append(op)
            else:
                s[1][eng] = op
        return op

    def barrier(self):
        lasts = []
        for e in self.ENG:
            for op in reversed(self.ops[e]):
                if op.dma_sem is None:
                    lasts.append(op)
                    break
        lasts += list(self.last_dma.values())
        for e in self.ENG:
            self.pending[e] = list(lasts)

    def emit(self, nc, block, sems):
        for e in self.ENG:
            c = 0
            for op in self.ops[e]:
                if op.dma_sem is None and op.marked:
                    c += 1
                op.semval = c
        sched = self

        def run(ename, eng):
            waited = {}
            for op in sched.ops[ename]:
                for d in op.deps:
                    if d.dma_sem is not None:
                        key, v = d.dma_sem, d.dma_val
                    else:
                        key, v = d.eng, d.semval
                    if waited.get(key, 0) < v:
                        eng.wait_ge(sems[key], v)
                        waited[key] = v
                ins = op.fn(eng)
                if op.dma_sem is not None:
                    ins.then_inc(sems[op.dma_sem], 16)
                elif op.marked:
                    ins.then_inc(sems[ename], 1)

        @block.tensor
        def _(e):
            run("pe", e)

        @block.scalar
        def _(e):
            run("act", e)

        @block.vector
        def _(e):
            run("dve", e)

        @block.gpsimd
        def _(e):
            run("pool", e)

        @block.sync
        def _(e):
            run("sp", e)


def _nat_row_info(qr):
    rs = min(max(qr - 4, 0), 24)
    mstart = rs // 2
    nslots = 5 if (rs % 2) else 4
    if 4 <= qr <= 28:
        cls = qr % 2
    elif qr < 4:
        cls = 2 + qr
    else:
        cls = 6 + (qr - 29)
    return cls, rs, mstart, nslots


_NAT_CLS_REP = [4, 5, 0, 1, 2, 3, 29, 30, 31]


def build_nc(nseq, debug=None):
    nc = bass.Bass("TRN2", target_bir_lowering=False)
    SC = Sched()

    def din(name, shape, dt=F32):
        return nc.dram_tensor(name, list(shape), dt, kind="ExternalInput").ap()

    x_d = din("x", [nseq, S, D])
    y_d = nc.dram_tensor("y", [nseq, S, D], F32, kind="ExternalOutput").ap()
    wmix_d = din("wmix", [2, D, NMIX])
    wbr_d = din("wbr", [2, 8, 1024, 128])
    wout_d = din("wout", [2, D, D])
    wfi_d = din("wfi", [2, 11, D, 512])
    wfo_d = din("wfo", [2, FFN, D])
    cw_d = din("cw", [2, 4, 128, 512])
    norms_d = din("norms", [5, D])
    lamv_d = din("lamv", [2, 128])
    pvec_d = din("pvec", [128, 2 * PV_L + 6])
    rope_d = din("rope", [4, 128, S])
    nb_d = din("nbias", [2, 4, 128, 9 * 320])
    cst_d = din("cst", [3, 128, 128])
    dbg_d = {}
    if debug:
        for nm, shp in debug.items():
            dbg_d[nm] = nc.dram_tensor("dbg_" + nm, list(shp), F32, kind="ExternalOutput").ap()

    es = ExitStack()

    def sb(name, shape, dt):
        return es.enter_context(nc.sbuf_tensor(name, list(shape), dt))

    X = sb("X", [128, NT * D], F32)
    HT = sb("HT", [128, 8 * S], BF16)
    R = sb("R", [128, 18432], F32)
    W = sb("W", [128, 2 * 4096], BF16)
    PT = sb("PT", [128, 6 * 512], BF16)
    TMP = sb("TMP", [128, 3 * 512], F32)
    GB = sb("GB", [128, D], F32)
    HB = sb("HB", [128, 2 * D], BF16)
    PV = sb("PV", [128, 2 * PV_L + 6], F32)
    ST = sb("ST", [128, 64], F32)
    LQ = sb("LQ", [128, 128], F32)
    CST = sb("CST", [128, 3 * 128], BF16)
    PSP = [es.enter_context(nc.psum_tensor("psp%d" % i, [128, 1024], F32)) for i in range(4)]
    PS = [PSP[i // 2][:, (i % 2) * 512:(i % 2) * 512 + 512] for i in range(8)]

    dma_sems = ["w0", "w1", "x0", "x1", "x2", "x3", "tblc", "tbls", "gb", "cst", "pv", "out0", "out1", "lq", "dbg", "wb0", "wb1", "nb1"]
    sems = {}
    for nm in list(Sched.ENG[:4]) + dma_sems:
        sems[nm] = es.enter_context(nc.semaphore("s_" + nm))

    X3 = X[:, :].rearrange("p (t d) -> p t d", d=D)
    HT3 = HT[:, :].rearrange("p (k s) -> p k s", s=S)
    ident = CST[:, 0:128]
    ones_b = CST[:, 128:256]
    bones = CST[:, 256:384]

    OT_OFF, S_OFF, TBL_OFF = 0, 32768, 61440

    def rview(off, nel, dt):
        sz = 2 if dt == BF16 else 4
        a = R[:, off // 4:(off + nel * sz) // 4]
        return a.bitcast(dt) if dt != F32 else a

    def rkeys(off, nbytes):
        return [("R", i) for i in range(off // 1024, (off + nbytes + 1023) // 1024)]

    def slab(off):
        return rview(off, S, BF16)

    def slab_keys(off, c=None):
        if c is None:
            return rkeys(off, 4096)
        return rkeys(off + c * 1024, 1024)

    OTs = [OT_OFF + 4096 * i for i in range(8)]
    MT_OFF = S_OFF

    def HBK(i):
        return [("QZ", 0, 2 * i), ("QZ", 0, 2 * i + 1)]

    GBK = [("QZ", 1, b_) for b_ in range(4)]

    def GBWK(g_):
        return [("QZ", 1, 2 * g_), ("QZ", 1, 2 * g_ + 1)]

    qz_ctr = [0]

    def make_qz(Qs, qoff, c, nblk):
        slot = qz_ctr[0] % 2
        qz_ctr[0] += 1
        base = HB[:, 0:2048] if slot == 0 else GB[:, :].bitcast(BF16)
        qz = base.rearrange("p (b c) -> p b c", c=512)
        mcol0 = 2 * PV_L + (0 if nblk == 4 else 4)
        keys = []
        for b_ in range(nblk):
            k = ("QZ", slot, b_)
            keys.append(k)
            ts(qz[:, b_, :], Qs[:, c * 512:(c + 1) * 512], PV[:, mcol0 + b_:mcol0 + b_ + 1], None, ALU.mult, None,
               slab_keys(qoff, c) + [("PV",)], [k])
        return qz, keys

    bank_ctr = [0]

    live_acc = set()
    cooling = []

    def nb():
        while True:
            b = bank_ctr[0] % 8
            bank_ctr[0] += 1
            if b not in live_acc and b not in cooling:
                return b

    def release_acc(b):
        live_acc.discard(b)
        cooling.append(b)
        if len(cooling) > 2:
            cooling.pop(0)

    pt_ctr = [0]

    def npt():
        i = pt_ctr[0] % 6
        pt_ctr[0] += 1
        return i

    def PTv(i, n=512):
        return PT[:, i * 512:i * 512 + n]

    def TMPv(i, n=512):
        return TMP[:, i * 512:i * 512 + n]

    ev_ctr = [0]

    def mm(out, lhsT, rhs, start, stop, reads, writes, tp=None):
        if tp is None:
            SC.add("pe", lambda e: e.matmul(out, lhsT, rhs, start=start, stop=stop), reads, writes)
        else:
            SC.add("pe", lambda e: e.matmul(out, lhsT, rhs, start=start, stop=stop, tile_position=tp), reads, writes)

    def act(out, in_, func, reads, writes, **kw):
        SC.add("act", lambda e: e.activation(out=out, in_=in_, func=func, **kw), reads, writes)

    def copy_any(out, in_, reads, writes):
        ev_ctr[0] += 1
        if ev_ctr[0] % 2:
            SC.add("act", lambda e: e.activation(out=out, in_=in_, func=AF.Copy), reads, writes)
        else:
            SC.add("dve", lambda e: e.tensor_copy(out=out, in_=in_), reads, writes)

    def tt(out, in0, in1, op, reads, writes):
        SC.add("dve", lambda e: e.tensor_tensor(out=out, in0=in0, in1=in1, op=op), reads, writes)

    def stt(out, in0, scalar, in1, op0, op1, reads, writes):
        SC.add("dve", lambda e: e.scalar_tensor_tensor(out=out, in0=in0, scalar=scalar, in1=in1, op0=op0, op1=op1),
               reads, writes)

    def ts(out, in0, s1, s2, op0, op1, reads, writes):
        if s2 is None:
            SC.add("dve", lambda e: e.tensor_scalar(out=out, in0=in0, scalar1=s1, scalar2=None, op0=op0), reads, writes)
        else:
            SC.add("dve", lambda e: e.tensor_scalar(out=out, in0=in0, scalar1=s1, scalar2=s2, op0=op0, op1=op1),
                   reads, writes)

    def recip(out, in_, reads, writes):
        SC.add("dve", lambda e: e.reciprocal(out=out, in_=in_), reads, writes)

    def memset(ap, val, writes):
        SC.add("dve", lambda e: e.memset(ap, val), (), writes)

    def dma(q, out, in_, sem, reads, writes):
        SC.add(q, lambda e: e.dma_start(out=out, in_=in_), reads, writes, dma_sem=sem)

    def rstd_from(out, in_, n, reads, writes):
        SC.add("act", lambda e: e.activation(out=out, in_=in_, func=AF.Ln, scale=1.0 / n, bias=EPSC),
               list(reads) + [("EPS",)], writes)
        SC.add("act", lambda e: e.activation(out=out, in_=out, func=AF.Exp, scale=-0.5), writes, writes)

    def recip_act(out, in_, reads, writes):
        SC.add("act", lambda e: e.activation(out=out, in_=in_, func=AF.Ln), reads, writes)
        SC.add("act", lambda e: e.activation(out=out, in_=out, func=AF.Exp, scale=-1.0), writes, writes)

    EPSC = ST[:, 40:41]

    w_ctr = [0]

    def load_w(src, nk, ncols):
        slot = w_ctr[0] % 2
        w_ctr[0] += 1
        view = W[:, slot * 4096:slot * 4096 + nk * ncols].rearrange("p (k c) -> p k c", c=ncols)
        key = ("W", slot)
        dma("pool", view, src.rearrange("(k p) c -> p k c", p=128), "w%d" % slot, (), [key])
        return view, key

    def dbg_dump(name, ap, reads):
        if debug and name in dbg_d:
            dma("sp", dbg_d[name], ap, "dbg", reads, ())

    def load_consts():
        dma("pool", CST[:, :].rearrange("p (a c) -> p a c", c=128), cst_d.rearrange("a p c -> p a c"), "cst", (), [("CST",)])
        dma("sp", PV[:, :], pvec_d, "pv", (), [("PV",)])
        memset(EPSC, EPS, [("EPS",)])

    def rmsnorm_to_HT(norm_idx):
        dma("sp", GB[:, :], norms_d[norm_idx].partition_broadcast(128), "gb", (), GBK)
        for t in range(NT):
            s = t % 2
            hb = HB[:, s * D:(s + 1) * D]
            SC.add("act", lambda e, hb=hb, t=t: e.activation(out=hb, in_=X3[:, t, :], func=AF.Square,
                                                             accum_out=ST[:, t:t + 1]),
                   [("X", t)], HBK(s) + [("ss", t)])
        rstd_from(ST[:, 16:32], ST[:, 0:16], D, [("ss", t) for t in range(NT)], [("rs",)])
        for t in range(NT):
            s = t % 2
            hb = HB[:, s * D:(s + 1) * D]
            stt(hb, X3[:, t, :], ST[:, 16 + t:17 + t], GB[:, :], ALU.mult, ALU.mult,
                [("X", t), ("rs",)] + GBK, HBK(s))
            b = nb()
            psb = PS[b].bitcast(BF16)
            for kc in range(8):
                SC.add("pe", lambda e, psb=psb, hb=hb, kc=kc: e.transpose(psb[:, kc * 128:(kc + 1) * 128],
                                                                          hb[:, kc * 128:(kc + 1) * 128], ident),
                       HBK(s) + [("CST",)], [("ps", b)])
            c = t // 4
            copy_any(HT3[:, :, t * 128:(t + 1) * 128], psb.rearrange("p (k c) -> p k c", c=128),
                     [("ps", b)], [("HT", kc, c) for kc in range(8)])

    def proj_fm(wv, wkey, col0, M, c, bank):
        for kc in range(8):
            mm(PS[bank][0:M, :], wv[:, kc, col0:col0 + M], HT3[:, kc, c * 512:(c + 1) * 512],
               kc == 0, kc == 7, [wkey, ("HT", kc, c)], [("ps", bank)])

    def load_tbl(which, c):
        cosv = rview(TBL_OFF, 512, F32)
        sinv = rview(TBL_OFF + 2048, 512, F32)
        dma("sp", cosv, rope_d[2 * which, :, c * 512:(c + 1) * 512], "tblc", (), rkeys(TBL_OFF, 2048))
        dma("sp", sinv, rope_d[2 * which + 1, :, c * 512:(c + 1) * 512], "tbls", (), rkeys(TBL_OFF + 2048, 2048))
        return cosv, sinv, rkeys(TBL_OFF, 4096)

    def rope_out(dst, dkeys, by, byp, rows, tbl, g=None, gp=None, rs=None):
        cosv, sinv, tk = tbl
        r0, r1 = rows
        t1 = TMPv(0)[r0:r1, :]
        t2 = TMPv(1)[r0:r1, :]
        if g is None:
            tt(t1, PS[by][r0:r1, :], cosv[r0:r1, :], ALU.mult, [("ps", by)] + tk, [("TMP", 0)])
            tt(t2, PS[byp][r0:r1, :], sinv[r0:r1, :], ALU.mult, [("ps", byp)] + tk, [("TMP", 1)])
        else:
            stt(t1, PS[by][r0:r1, :], g[r0:r1, :], cosv[r0:r1, :], ALU.mult, ALU.mult, [("ps", by), ("PV",)] + tk, [("TMP", 0)])
            stt(t2, PS[byp][r0:r1, :], gp[r0:r1, :], sinv[r0:r1, :], ALU.mult, ALU.mult, [("ps", byp), ("PV",)] + tk, [("TMP", 1)])
        if rs is None:
            tt(dst, t1, t2, ALU.add, [("TMP", 0), ("TMP", 1)], dkeys)
        else:
            tt(t1, t1, t2, ALU.add, [("TMP", 0), ("TMP", 1)], [("TMP", 0)])
            tt(dst, t1, rs[r0:r1, :], ALU.mult, [("TMP", 0), ("TMP", 2)], dkeys)

    def v_token_major(wv, wkey, ncol, dst3, dst_off, groups):
        for t in range(NT):
            b = nb()
            for kc in range(8):
                mm(PS[b][:, 0:ncol], HT3[:, kc, t * 128:(t + 1) * 128], wv[:, kc, 0:ncol], kc == 0, kc == 7,
                   [wkey, ("HT", kc, t // 4)], [("ps", b)])
            for (s0, n, d0, dstride) in groups:
                src = PS[b][:, s0:s0 + 64 * n].rearrange("p (a c) -> p a c", c=64)
                width = dst3.shape[2]
                dst = dst3[:, t, d0:d0 + dstride * n].rearrange("p (a c) -> p a c", c=dstride)[:, :, 0:64] \
                    if dstride * n + d0 <= width else None
                if dst is None:
                    for a in range(n):
                        copy_any(dst3[:, t, d0 + a * dstride:d0 + a * dstride + 64], PS[b][:, s0 + 64 * a:s0 + 64 * a + 64],
                                 [("ps", b)], rkeys(dst_off + t * width * 2, width * 2))
                else:
                    copy_any(dst, src, [("ps", b)], rkeys(dst_off + t * width * 2, width * 2))

    acc_ctr = [0]
    pair_ctr = [0]
    pt2_ctr = [0]

    def attn_core(q_ap, qkeys, k_of, kkeys_of, v_of, vkeys_of, scale, tp, la=2):
        ob = 6 + (acc_ctr[0] % 2)
        acc_ctr[0] += 1
        live_acc.add(ob)
        slots = {}
        NP = NT // 2

        def score(p):
            pr = pair_ctr[0] % 3
            pair_ctr[0] += 1
            for hf in range(2):
                kt = 2 * p + hf
                mm(PS[2 * pr + hf][:, :], k_of(kt), q_ap, True, True, qkeys + kkeys_of(kt), [("ps", 2 * pr + hf)], tp=tp)
            sl = pt2_ctr[0] % 3
            pt2_ctr[0] += 1
            act(PT[:, sl * 1024:(sl + 1) * 1024], PSP[pr][:, :], AF.Exp, [("ps", 2 * pr), ("ps", 2 * pr + 1)],
                [("PT", 2 * sl), ("PT", 2 * sl + 1)], scale=scale)
            slots[p] = sl

        for i in range(NP + la):
            if i < NP:
                score(i)
            p = i - la
            if p >= 0:
                sl = slots[p]
                for hf in range(2):
                    kt = 2 * p + hf
                    mm(PS[ob][:, :], v_of(kt), PT[:, sl * 1024 + hf * 512:sl * 1024 + hf * 512 + 512], kt == 0, kt == NT - 1,
                       [("PT", 2 * sl + hf)] + vkeys_of(kt), [("ps", ob)])
        return ob

    def norm_rows(hpar):
        return ((0, 64), (64, 128)) if hpar == 0 else ((64, 128), (0, 64))

    def mixer_A(l, lam_init):
        Qo = [S_OFF, S_OFF + 4096]
        Ko = [S_OFF + 8192, S_OFF + 12288]
        Vo = S_OFF + 16384
        V3 = rview(Vo, NT * 384, BF16).rearrange("p (t c) -> p t c", c=384)
        memset(V3[:, :, 64:128], 1.0, rkeys(Vo, 12288))
        memset(V3[:, :, 256:320], 1.0, rkeys(Vo, 12288))
        dma("sp", LQ[:, :], lamv_d[l].partition_broadcast(128), "lq", (), [("LQ",)])
        tt(LQ[:, 0:32], LQ[:, 0:32], LQ[:, 32:64], ALU.mult, [("LQ",)], [("LQ",)])
        tt(LQ[:, 64:96], LQ[:, 64:96], LQ[:, 96:128], ALU.mult, [("LQ",)], [("LQ",)])
        SC.add("dve", lambda e: e.reduce_sum(out=ST[:, 32:33], in_=LQ[:, 0:32], axis=AX.X), [("LQ",)], [("lam",)])
        SC.add("dve", lambda e: e.reduce_sum(out=ST[:, 33:34], in_=LQ[:, 64:96], axis=AX.X), [("LQ",)], [("lam",)])
        act(ST[:, 34:36], ST[:, 32:34], AF.Exp, [("lam",)], [("lam",)])
        tt(ST[:, 36:37], ST[:, 35:36], ST[:, 34:35], ALU.subtract, [("lam",)], [("lam",)])
        ts(ST[:, 36:37], ST[:, 36:37], -lam_init, None, ALU.add, None, [("lam",)], [("lam",)])
        neglam = ST[:, 36:37]
        ts(ST[:, 37:38], PV[:, l * PV_L + PV_SUB:l * PV_L + PV_SUB + 1], 1.0 - lam_init, None, ALU.mult, None,
           [("PV",)], [("subg",)])
        subg = ST[:, 37:38]

        for (dsts, col0) in ((Qo, 0), (Ko, 512)):
            wv, wk = load_w(wmix_d[l][:, A0 + col0:A0 + col0 + 512], 8, 512)
            for c in range(4):
                tbl = load_tbl(0, c)
                for j in range(2):
                    b1, b2 = nb(), nb()
                    proj_fm(wv, wk, j * 128, 128, c, b1)
                    proj_fm(wv, wk, 256 + j * 128, 128, c, b2)
                    rope_out(slab(dsts[j])[:, c * 512:(c + 1) * 512], slab_keys(dsts[j], c), b1, b2, (0, 128), tbl)
        wv, wk = load_w(wmix_d[l][:, A0 + 1024:A0 + 1280], 8, 256)
        v_token_major(wv, wk, 256, V3, Vo, [(0, 2, 0, 128), (128, 2, 192, 128)])
        voffs = [0, 64, 192, 256]

        jc = [(j_, c_) for j_ in range(2) for c_ in range(4)]
        pending_subln = [None]
        qz_next = make_qz(slab(Qo[0]), Qo[0], 0, 4)
        for ji, (j, c) in enumerate(jc):
            Qs, Ks = slab(Qo[j]), slab(Ko[j])
            if True:
                qz, qzk = qz_next
                for hh in range(2):
                    h = 2 * j + hh
                    orow, drow = norm_rows(hh)
                    for comp in range(2):
                        blk = hh * 2 + comp
                        if blk == 1 and ji + 1 < len(jc):
                            jn, cn = jc[ji + 1]
                            qz_next = make_qz(slab(Qo[jn]), Qo[jn], cn, 4)
                        ob = attn_core(
                            qz[:, blk, :], [qzk[blk]],
                            lambda kt, Ks=Ks: Ks[:, kt * 128:(kt + 1) * 128],
                            lambda kt, j=j: slab_keys(Ko[j], kt // 4),
                            lambda kt, h=h: V3[:, kt, voffs[h]:voffs[h] + 128],
                            lambda kt: rkeys(Vo + kt * 768, 768),
                            32 ** -0.5, None)
                        if blk == 0 and pending_subln[0] is not None:
                            pending_subln[0]()
                            pending_subln[0] = None
                        tcomp = TMPv(comp)
                        recip(tcomp[orow[0]:orow[1], :], PS[ob][drow[0]:drow[1], :], [("ps", ob)], [("TMP", comp)])
                        tt(tcomp[orow[0]:orow[1], :], tcomp[orow[0]:orow[1], :], PS[ob][orow[0]:orow[1], :], ALU.mult,
                           [("ps", ob), ("TMP", comp)], [("TMP", comp)])
                        release_acc(ob)
                    stt(TMPv(0)[orow[0]:orow[1], :], TMPv(1)[orow[0]:orow[1], :], neglam[orow[0]:orow[1], :],
                        TMPv(0)[orow[0]:orow[1], :], ALU.mult, ALU.add, [("TMP", 0), ("TMP", 1), ("lam",)], [("TMP", 0)])
                def subln(j=j, c=c):
                    pi = npt()
                    act(PTv(pi), TMPv(0), AF.Square, [("TMP", 0)], [("PT", pi)])
                    b = nb()
                    mm(PS[b][:, :], bones, PTv(pi), True, True, [("PT", pi), ("CST",)], [("ps", b)])
                    rstd_from(TMPv(2), PS[b][:, :], 64, [("ps", b)], [("TMP", 2)])
                    stt(slab(OTs[0 + j])[:, c * 512:(c + 1) * 512], TMPv(0), subg, TMPv(2), ALU.mult, ALU.mult,
                        [("TMP", 0), ("TMP", 2), ("subg",)], slab_keys(OTs[0 + j], c))
                pending_subln[0] = subln
        pending_subln[0]()

    def mixer_B(l):
        Qo = [S_OFF, S_OFF + 4096]
        Ko = [S_OFF + 8192, S_OFF + 12288]
        Vo = S_OFF + 16384
        V3 = rview(Vo, NT * 384, BF16).rearrange("p (t c) -> p t c", c=384)
        memset(V3[:, :, 64:128], 1.0, rkeys(Vo, 12288))
        memset(V3[:, :, 256:320], 1.0, rkeys(Vo, 12288))
        wv, wk = load_w(wmix_d[l][:, B0:B0 + 512], 8, 512)
        for c in range(4):
            for j in range(4):
                b = nb()
                proj_fm(wv, wk, j * 128, 128, c, b)
                off = (Qo + Ko)[j]
                copy_any(slab(off)[:, c * 512:(c + 1) * 512], PS[b][:, :], [("ps", b)], slab_keys(off, c))
        wv, wk = load_w(wmix_d[l][:, B0 + 512:B0 + 768], 8, 256)
        v_token_major(wv, wk, 256, V3, Vo, [(0, 2, 0, 128), (128, 2, 192, 128)])
        voffs = [0, 64, 192, 256]
        for h in range(4):
            j, hh = h // 2, h % 2
            base = 64 * hh
            orow, drow = norm_rows(hh)
            Qs, Ks = slab(Qo[j]), slab(Ko[j])
            boff = TBL_OFF if h % 2 == 0 else OTs[4]
            tblv = rview(boff, 9 * 320, F32)
            tkeys = rkeys(boff, 9 * 320 * 4)
            dma("sp", tblv, nb_d[l, h], "tblc" if h % 2 == 0 else "nb1", (), tkeys)
            LA = 3
            info = {}
            obs = {}

            def stage1(qr, Qs=Qs, Ks=Ks, base=base, j=j, tblv=tblv, tkeys=tkeys):
                cls, rs, mstart, ns = _nat_row_info(qr)
                sbk = nb()
                for s_ in range(ns):
                    m = mstart + s_
                    mm(PS[sbk][:, s_ * 64:(s_ + 1) * 64], Ks[base:base + 64, m * 128:(m + 1) * 128],
                       Qs[base:base + 64, qr * 64:(qr + 1) * 64], True, True,
                       slab_keys(Qo[j], qr // 8) + slab_keys(Ko[j], m // 4), [("ps", sbk)])
                n = ns * 64
                stt(PS[sbk][:, 0:n], PS[sbk][:, 0:n], 0.125, tblv[:, cls * 320:cls * 320 + n], ALU.mult, ALU.add,
                    [("ps", sbk)] + tkeys, [("ps", sbk)])
                pi = npt()
                act(PTv(pi)[:, 0:n], PS[sbk][:, 0:n], AF.Exp, [("ps", sbk)], [("PT", pi)])
                info[qr] = (pi, mstart, ns)

            def stage3(qr, h=h, j=j, orow=orow, drow=drow):
                qc, q8 = qr // 8, qr % 8
                if q8 == 0:
                    obs[qc] = nb()
                    live_acc.add(obs[qc])
                ob = obs[qc]
                pi, mstart, ns = info[qr]
                for s_ in range(ns):
                    m = mstart + s_
                    mm(PS[ob][:, q8 * 64:(q8 + 1) * 64], V3[:, m, voffs[h]:voffs[h] + 128], PTv(pi)[:, s_ * 64:(s_ + 1) * 64],
                       s_ == 0, s_ == ns - 1, [("PT", pi)] + rkeys(Vo + m * 768, 768), [("ps", ob)])
                if q8 == 7:
                    t0 = TMPv(1)
                    recip_act(t0[orow[0]:orow[1], :], PS[ob][drow[0]:drow[1], :], [("ps", ob)], [("TMP", 1)])
                    tt(slab(OTs[2 + j])[orow[0]:orow[1], qc * 512:(qc + 1) * 512], t0[orow[0]:orow[1], :],
                       PS[ob][orow[0]:orow[1], :], ALU.mult, [("ps", ob), ("TMP", 1)], slab_keys(OTs[2 + j], qc))
                    release_acc(ob)

            for i in range(32 + LA):
                if i < 32:
                    stage1(i)
                if i >= LA:
                    stage3(i - LA)

    def mixer_C(l):
        CQ0, CQ1, CKV, KR, QH, KH, VH = [S_OFF + 4096 * i for i in range(7)]
        pv = l * PV_L
        wv, wk = load_w(wmix_d[l][:, C0:C0 + 512], 8, 512)
        for c in range(4):
            cs = slice(c * 512, (c + 1) * 512)
            b0, b1, b2 = nb(), nb(), nb()
            proj_fm(wv, wk, 0, 128, c, b0)
            proj_fm(wv, wk, 128, 64, c, b1)
            proj_fm(wv, wk, 192, 128, c, b2)
            p0, p1 = npt(), npt()
            act(PTv(p0), PS[b0][:, :], AF.Square, [("ps", b0)], [("PT", p0)])
            act(PTv(p1)[0:64, :], PS[b1][0:64, :], AF.Square, [("ps", b1)], [("PT", p1)])
            b3 = nb()
            mm(PS[b3][:, :], ones_b, PTv(p0), True, False, [("PT", p0), ("CST",)], [("ps", b3)])
            mm(PS[b3][:, :], ones_b[0:64, :], PTv(p1)[0:64, :], False, True, [("PT", p1), ("CST",)], [("ps", b3)])
            rstd_from(TMPv(2), PS[b3][:, :], 192, [("ps", b3)], [("TMP", 2)])
            stt(slab(CQ0)[:, cs], PS[b0][:, :], PV[:, pv + PV_CQ0:pv + PV_CQ0 + 1], TMPv(2), ALU.mult, ALU.mult,
                [("ps", b0), ("TMP", 2), ("PV",)], slab_keys(CQ0, c))
            stt(slab(CQ1)[0:64, cs], PS[b1][0:64, :], PV[0:64, pv + PV_CQ1:pv + PV_CQ1 + 1], TMPv(2)[0:64, :], ALU.mult, ALU.mult,
                [("ps", b1), ("TMP", 2), ("PV",)], slab_keys(CQ1, c))
            p2 = npt()
            act(PTv(p2), PS[b2][:, :], AF.Square, [("ps", b2)], [("PT", p2)])
            b4 = nb()
            mm(PS[b4][:, :], ones_b, PTv(p2), True, True, [("PT", p2), ("CST",)], [("ps", b4)])
            rstd_from(TMPv(2), PS[b4][:, :], 128, [("ps", b4)], [("TMP", 2)])
            stt(slab(CKV)[:, cs], PS[b2][:, :], PV[:, pv + PV_CKV:pv + PV_CKV + 1], TMPv(2), ALU.mult, ALU.mult,
                [("ps", b2), ("TMP", 2), ("PV",)], slab_keys(CKV, c))
            tbl = load_tbl(0, c)
            b5, b6 = nb(), nb()
            proj_fm(wv, wk, 320, 96, c, b5)
            proj_fm(wv, wk, 416, 96, c, b6)
            rope_out(slab(KR)[64:96, cs], slab_keys(KR, c), b5, b6, (64, 96), tbl)
        for h in range(4):
            hh = h % 2
            orow, drow = norm_rows(hh)
            wslot = w_ctr[0] % 2
            w_ctr[0] += 1
            wc = W[:, wslot * 4096:wslot * 4096 + 512]
            wk = ("W", wslot)
            dma("pool", wc, cw_d[l, h], "w%d" % wslot, (), [wk])
            VH3 = rview(VH, NT * 128, BF16).rearrange("p (t c) -> p t c", c=128)
            vo = 0 if hh == 0 else 64
            memset(VH3[:, :, 64 - vo:128 - vo], 1.0, rkeys(VH, 4096))
            for half in range(2):
                b = nb()
                for i in range(8):
                    t = half * 8 + i
                    mm(PS[b][:, i * 64:(i + 1) * 64], slab(CKV)[:, t * 128:(t + 1) * 128], wc[:, 448:512], True, True,
                       [wk] + slab_keys(CKV, t // 4), [("ps", b)])
                copy_any(VH3[:, half * 8:(half + 1) * 8, vo:vo + 64], PS[b][:, :].rearrange("p (a c) -> p a c", c=64),
                         [("ps", b)], rkeys(VH, 4096))
            for c in range(4):
                cs = slice(c * 512, (c + 1) * 512)
                b2 = nb()
                mm(PS[b2][0:64, :], wc[:, 384:448], slab(CKV)[:, cs], True, True, [wk] + slab_keys(CKV, c), [("ps", b2)])
                copy_any(slab(KH)[0:64, cs], PS[b2][0:64, :], [("ps", b2)], slab_keys(KH, c))
                copy_any(slab(KH)[64:96, cs], slab(KR)[64:96, cs], slab_keys(KR, c), slab_keys(KH, c))

            def q_prologue(c, wc=wc, wk=wk):
                cs = slice(c * 512, (c + 1) * 512)
                tbl = load_tbl(0, c)
                b0, b1 = nb(), nb()
                mm(PS[b0][0:96, :], wc[:, 0:96], slab(CQ0)[:, cs], True, False, [wk] + slab_keys(CQ0, c), [("ps", b0)])
                mm(PS[b0][0:96, :], wc[0:64, 192:288], slab(CQ1)[0:64, cs], False, True, [wk] + slab_keys(CQ1, c), [("ps", b0)])
                mm(PS[b1][0:96, :], wc[:, 96:192], slab(CQ0)[:, cs], True, False, [wk] + slab_keys(CQ0, c), [("ps", b1)])
                mm(PS[b1][0:96, :], wc[0:64, 288:384], slab(CQ1)[0:64, cs], False, True, [wk] + slab_keys(CQ1, c), [("ps", b1)])
                copy_any(slab(QH)[0:64, cs], PS[b0][0:64, :], [("ps", b0)], slab_keys(QH, c))
                rope_out(slab(QH)[64:96, cs], slab_keys(QH, c), b0, b1, (64, 96), tbl)

            q_prologue(0)
            for c in range(4):
                if c + 1 < 4:
                    q_prologue(c + 1)
                ob = attn_core(
                    slab(QH)[0:96, c * 512:(c + 1) * 512], slab_keys(QH, c),
                    lambda kt: slab(KH)[0:96, kt * 128:(kt + 1) * 128],
                    lambda kt: slab_keys(KH, kt // 4),
                    lambda kt: VH3[:, kt, :],
                    lambda kt: rkeys(VH, 4096),
                    96 ** -0.5, None)
                t0 = TMPv(2)
                recip_act(t0[orow[0]:orow[1], :], PS[ob][drow[0]:drow[1], :], [("ps", ob)], [("TMP", 2)])
                tt(slab(OTs[4 + h // 2])[orow[0]:orow[1], c * 512:(c + 1) * 512], t0[orow[0]:orow[1], :],
                   PS[ob][orow[0]:orow[1], :], ALU.mult, [("ps", ob), ("TMP", 2)], slab_keys(OTs[4 + h // 2], c))
                release_acc(ob)

    def mixer_D(l):
        Qo = [S_OFF, S_OFF + 4096]
        Ko = [S_OFF + 8192, S_OFF + 12288]
        Vo = S_OFF + 16384
        V3 = rview(Vo, NT * 384, BF16).rearrange("p (t c) -> p t c", c=384)
        memset(V3[:, :, 64:128], 1.0, rkeys(Vo, 12288))
        memset(V3[:, :, 256:320], 1.0, rkeys(Vo, 12288))
        pv = l * PV_L
        for (dsts, col0, gc, gpc) in ((Qo, 0, PV_DQ, PV_DQP), (Ko, 512, PV_DK, PV_DKP)):
            wv, wk = load_w(wmix_d[l][:, D0 + col0:D0 + col0 + 512], 8, 512)
            for c in range(4):
                tbl = load_tbl(1, c)
                for j in range(2):
                    b1, b2 = nb(), nb()
                    proj_fm(wv, wk, j * 128, 128, c, b1)
                    proj_fm(wv, wk, 256 + j * 128, 128, c, b2)
                    pi = npt()
                    act(PTv(pi), PS[b1][:, :], AF.Square, [("ps", b1)], [("PT", pi)])
                    b3 = nb()
                    mm(PS[b3][:, :], bones, PTv(pi), True, True, [("PT", pi), ("CST",)], [("ps", b3)])
                    rstd_from(TMPv(2), PS[b3][:, :], 64, [("ps", b3)], [("TMP", 2)])
                    rope_out(slab(dsts[j])[:, c * 512:(c + 1) * 512], slab_keys(dsts[j], c), b1, b2, (0, 128), tbl,
                             g=PV[:, pv + gc:pv + gc + 1], gp=PV[:, pv + gpc:pv + gpc + 1], rs=TMPv(2))
        wv, wk = load_w(wmix_d[l][:, D0 + 1024:D0 + 1152], 8, 128)
        v_token_major(wv, wk, 128, V3, Vo, [(0, 1, 0, 128), (0, 1, 128, 128), (64, 1, 192, 128), (64, 1, 320, 128)])
        voffs = [0, 64, 192, 256]
        jc = [(j_, c_) for j_ in range(2) for c_ in range(4)]
        qz_next = make_qz(slab(Qo[0]), Qo[0], 0, 2)
        for ji, (j, c) in enumerate(jc):
            Qs, Ks = slab(Qo[j]), slab(Ko[j])
            if True:
                qz, qzk = qz_next
                for hh in range(2):
                    h = 2 * j + hh
                    orow, drow = norm_rows(hh)
                    if hh == 1 and ji + 1 < len(jc):
                        jn, cn = jc[ji + 1]
                        qz_next = make_qz(slab(Qo[jn]), Qo[jn], cn, 2)
                    ob = attn_core(
                        qz[:, hh, :], [qzk[hh]],
                        lambda kt, Ks=Ks: Ks[:, kt * 128:(kt + 1) * 128],
                        lambda kt, j=j: slab_keys(Ko[j], kt // 4),
                        lambda kt, h=h: V3[:, kt, voffs[h]:voffs[h] + 128],
                        lambda kt: rkeys(Vo + kt * 768, 768),
                        0.125, None)
                    t0 = TMPv(1)
                    recip(t0[orow[0]:orow[1], :], PS[ob][drow[0]:drow[1], :], [("ps", ob)], [("TMP", 1)])
                    tt(slab(OTs[6 + j])[orow[0]:orow[1], c * 512:(c + 1) * 512], t0[orow[0]:orow[1], :],
                       PS[ob][orow[0]:orow[1], :], ALU.mult, [("ps", ob), ("TMP", 1)], slab_keys(OTs[6 + j], c))
                    release_acc(ob)

    def merge_and_out(l):
        MT3 = rview(MT_OFF, 8 * S, BF16).rearrange("p (k s) -> p k s", s=S)
        for oc in range(8):
            wg, wgk = load_w(wmix_d[l][:, G0 + oc * 512:G0 + (oc + 1) * 512], 8, 512)
            gslot = oc % 2
            wb = GB[:, gslot * 512:(gslot + 1) * 512].bitcast(BF16).rearrange("p (k c) -> p k c", c=128)
            wbk = GBWK(gslot)
            dma("pool", wb, wbr_d[l, oc].rearrange("(k p) c -> p k c", p=128), "wb%d" % gslot, (), wbk)
            for c in range(4):
                cs = slice(c * 512, (c + 1) * 512)
                for i in range(4):
                    bg, bb = nb(), nb()
                    proj_fm(wg, wgk, i * 128, 128, c, bg)
                    for kc in range(2):
                        mm(PS[bb][:, :], wb[:, 2 * i + kc, :], slab(OTs[2 * i + kc])[:, cs], kc == 0, kc == 1,
                           wbk + slab_keys(OTs[2 * i + kc], c), [("ps", bb)])
                    act(TMPv(1), PS[bg][:, :], AF.Sigmoid, [("ps", bg)], [("TMP", 1)])
                    if i == 0:
                        tt(TMPv(0), TMPv(1), PS[bb][:, :], ALU.mult, [("TMP", 1), ("ps", bb)], [("TMP", 0)])
                    else:
                        tt(TMPv(1), TMPv(1), PS[bb][:, :], ALU.mult, [("TMP", 1), ("ps", bb)], [("TMP", 1)])
                        if i < 3:
                            tt(TMPv(0), TMPv(0), TMPv(1), ALU.add, [("TMP", 0), ("TMP", 1)], [("TMP", 0)])
                        else:
                            tt(MT3[:, oc, cs], TMPv(0), TMPv(1), ALU.add, [("TMP", 0), ("TMP", 1)],
                               rkeys(MT_OFF + oc * 4096 + c * 1024, 1024))
        for cg in range(2):
            wv, wk = load_w(wout_d[l][:, cg * 512:(cg + 1) * 512], 8, 512)
            for t in range(NT):
                b = nb()
                for kc in range(8):
                    mm(PS[b][:, :], MT3[:, kc, t * 128:(t + 1) * 128], wv[:, kc, :], kc == 0, kc == 7,
                       [wk] + rkeys(MT_OFF + kc * 4096 + (t // 4) * 1024, 1024), [("ps", b)])
                tt(X3[:, t, cg * 512:(cg + 1) * 512], X3[:, t, cg * 512:(cg + 1) * 512], PS[b][:, :], ALU.add,
                   [("ps", b), ("X", t)], [("X", t)])

    def ffn(l):
        AT_OFF = 0
        GU = []
        for i_ in range(2):
            g_off = 32768 + i_ * 16448
            u_off = g_off + 8256
            GU.append((rview(g_off, 2050, F32), rview(u_off, 2048, F32), rkeys(g_off, 8200), rkeys(u_off, 8192)))
        pv = l * PV_L + PV_CONV
        tmp_ctr = [0]
        AT3 = rview(AT_OFF, 8 * S, BF16).rearrange("p (k s) -> p k s", s=S)
        groups = ((0, 8), (8, 8), (16, 6))

        def stage_b1(f, fc, c):
            Gv, Uv, gk, uk = GU[fc % 2]
            w0 = PV[:, pv + 4 * fc + 0:pv + 4 * fc + 1]
            w1 = PV[:, pv + 4 * fc + 1:pv + 4 * fc + 2]
            w2 = PV[:, pv + 4 * fc + 2:pv + 4 * fc + 3]
            bb = PV[:, pv + 4 * fc + 3:pv + 4 * fc + 4]
            c0 = c * 512
            ti = tmp_ctr[0] % 3
            tmp_ctr[0] += 1
            T = TMPv(ti)
            tk = [("TMP", ti)]
            SC.add("act", lambda e, T=T, Gv=Gv, c0=c0, w1=w1, bb=bb: e.activation(
                out=T, in_=Gv[:, c0 + 1:c0 + 513], func=AF.Identity, scale=w1, bias=bb), gk + [("PV",)], tk)
            stt(T, Gv[:, c0:c0 + 512], w0, T, ALU.mult, ALU.add, gk + [("PV",)] + tk, tk)
            stt(T, Gv[:, c0 + 2:c0 + 514], w2, T, ALU.mult, ALU.add, gk + [("PV",)] + tk, tk)
            return T, tk

        def stage_b2(f, fc, c, T, tk):
            Gv, Uv, gk, uk = GU[fc % 2]
            c0 = c * 512
            act(T, T, AF.Silu, tk, tk)
            tt(AT3[:, f, c0:c0 + 512], T, Uv[:, c0:c0 + 512], ALU.mult, tk + uk,
               rkeys(AT_OFF + f * 4096 + c * 1024, 1024))

        def ffn_out(f0, nf):
            for cg in range(2):
                wv, wk = load_w(wfo_d[l][f0 * 128:(f0 + nf) * 128, cg * 512:(cg + 1) * 512], nf, 512)
                for t in range(NT):
                    b = nb()
                    for f in range(nf):
                        mm(PS[b][:, :], AT3[:, f, t * 128:(t + 1) * 128], wv[:, f, :], f == 0, f == nf - 1,
                           [wk] + rkeys(AT_OFF + f * 4096 + (t // 4) * 1024, 1024), [("ps", b)])
                    tt(X3[:, t, cg * 512:(cg + 1) * 512], X3[:, t, cg * 512:(cg + 1) * 512], PS[b][:, :], ALU.add,
                       [("ps", b), ("X", t)], [("X", t)])

        prev = None
        wv = wk = None
        for gi, (f0, nf) in enumerate(groups):
            for f in range(nf):
                fc = f0 + f
                if f % 2 == 0 or (f == 1 and gi > 0):
                    wv, wk = load_w(wfi_d[l, fc // 2], 8, 512)
                f2 = fc % 2
                Gv, Uv, gk, uk = GU[fc % 2]
                memset(Gv[:, 0:1], 0.0, gk)
                memset(Gv[:, 2049:2050], 0.0, gk)
                for c in range(4):
                    if prev is not None:
                        Tt = stage_b1(prev[0], prev[1], c)
                    bu, bg = nb(), nb()
                    proj_fm(wv, wk, f2 * 256, 128, c, bu)
                    proj_fm(wv, wk, f2 * 256 + 128, 128, c, bg)
                    act(Uv[:, c * 512:(c + 1) * 512], PS[bu][:, :], AF.Copy, [("ps", bu)], uk)
                    act(Gv[:, 1 + c * 512:1 + (c + 1) * 512], PS[bg][:, :], AF.Copy, [("ps", bg)], gk)
                    if prev is not None:
                        stage_b2(prev[0], prev[1], c, *Tt)
                prev = None
                if f == 0 and gi > 0:
                    ffn_out(*groups[gi - 1])
                prev = (f, fc)
                if f == 0 and gi > 0:
                    pass
        for c in range(4):
            Tt = stage_b1(prev[0], prev[1], c)
            stage_b2(prev[0], prev[1], c, *Tt)
        ffn_out(*groups[-1])

    def final_norm_store(si):
        dma("sp", GB[:, :], norms_d[4].partition_broadcast(128), "gb", (), GBK)
        for t in range(NT):
            s = t % 2
            hb = HB[:, s * D:(s + 1) * D]
            SC.add("act", lambda e, hb=hb, t=t: e.activation(out=hb, in_=X3[:, t, :], func=AF.Square,
                                                             accum_out=ST[:, t:t + 1]),
                   [("X", t)], HBK(s) + [("ss", t)])
        rstd_from(ST[:, 16:32], ST[:, 0:16], D, [("ss", t) for t in range(NT)], [("rs",)])
        for t in range(NT):
            s = t % 2
            ys = rview(s * 4096, D, F32)
            yk = rkeys(s * 4096, 4096)
            stt(ys, X3[:, t, :], ST[:, 16 + t:17 + t], GB[:, :], ALU.mult, ALU.mult, [("X", t), ("rs",)] + GBK, yk)
            dma("sp", y_d[si, t * 128:(t + 1) * 128, :], ys, "out%d" % s, yk, ())

    load_consts()
    for si in range(nseq):
        for i in range(4):
            dma("sp", X3[:, 4 * i:4 * i + 4, :], x_d[si, 512 * i:512 * (i + 1), :].rearrange("(t p) d -> p t d", p=128),
                "x%d" % i, (), [("X", t) for t in range(4 * i, 4 * i + 4)])
        for l in range(2):
            lam_init = 0.8 - 0.6 * math.exp(-0.3 * l)
            SC.phase = "norm1"
            rmsnorm_to_HT(l)
            SC.phase = "A"
            mixer_A(l, lam_init)
            SC.phase = "B"
            mixer_B(l)
            SC.phase = "C"
            mixer_C(l)
            SC.phase = "D"
            mixer_D(l)
            SC.phase = "merge"
            merge_and_out(l)
            SC.phase = "norm2"
            rmsnorm_to_HT(2 + l)
            SC.phase = "ffn"
            ffn(l)
        final_norm_store(si)
    SC.barrier()
    SC.add("sp", lambda e: e.nop(), (), ())

    nc._pe_phase = SC.pe_phase
    with nc.Block() as block:
        SC.emit(nc, block, sems)
    es.close()
    return nc


def _partner_perm(n):
    idx = np.arange(n)
    return (idx // 32) * 32 + ((idx % 32) + 16) % 32


def _prep_shared(inp):
    f32 = np.float32
    w_in = np.asarray(inp["w_in"], f32)
    L = 2
    wmix = np.zeros((L, D, NMIX), f32)
    aq, ak, av = 0, 256, 512
    bq, bk, bv = 768, 1024, 1280
    ccq, cckv, ckr = 1536, 1728, 1856
    dq, dk, dv = 1888, 2144, 2272
    g0 = 2400
    p256 = _partner_perm(256)
    p128 = _partner_perm(128)
    p32 = _partner_perm(32)
    for l in range(L):
        w = w_in[l]
        m = wmix[l]
        m[:, A0:A0 + 256] = w[:, aq:aq + 256]
        m[:, A0 + 256:A0 + 512] = w[:, aq + p256]
        m[:, A0 + 512:A0 + 768] = w[:, ak:ak + 256]
        m[:, A0 + 768:A0 + 1024] = w[:, ak + p256]
        m[:, A0 + 1024:A0 + 1280] = w[:, av:av + 256]
        m[:, B0:B0 + 768] = w[:, bq:bq + 768]
        m[:, C0:C0 + 320] = w[:, ccq:ccq + 320]
        m[:, C0 + 320:C0 + 384] = w[:, cckv:cckv + 64]
        m[:, C0 + 384:C0 + 416] = w[:, ckr:ckr + 32]
        m[:, C0 + 416:C0 + 480] = w[:, cckv:cckv + 64]
        m[:, C0 + 480:C0 + 512] = w[:, ckr + p32]
        m[:, D0:D0 + 256] = w[:, dq:dq + 256]
        m[:, D0 + 256:D0 + 512] = w[:, dq + p256]
        kd = np.concatenate([w[:, dk:dk + 64], w[:, dk:dk + 64], w[:, dk + 64:dk + 128], w[:, dk + 64:dk + 128]], axis=1)
        kp = w[:, dk + p128]
        kdp = np.concatenate([kp[:, 0:64], kp[:, 0:64], kp[:, 64:128], kp[:, 64:128]], axis=1)
        m[:, D0 + 512:D0 + 768] = kd
        m[:, D0 + 768:D0 + 1024] = kdp
        m[:, D0 + 1024:D0 + 1152] = w[:, dv:dv + 128]
        gates = w[:, g0:g0 + 4096].reshape(D, 4, 8, 128).transpose(0, 2, 1, 3).reshape(D, 4096)
        m[:, G0:G0 + 4096] = gates
    wbr = np.asarray(inp["w_branch"], f32)
    wbr2 = wbr.reshape(L, 4, 256, 8, 128).transpose(0, 3, 1, 2, 4).reshape(L, 8, 1024, 128)
    wfi = np.asarray(inp["w_ffn_in"], f32)
    u = wfi[:, :, :FFN].reshape(L, D, NFC, 128)
    g = wfi[:, :, FFN:].reshape(L, D, NFC, 128)
    ug = np.stack([u, g], axis=3)
    wfi2 = ug.reshape(L, D, 11, 512).transpose(0, 2, 1, 3)
    wuq = np.asarray(inp["c_w_uq"], f32)
    wukv = np.asarray(inp["c_w_ukv"], f32)
    cw = np.zeros((L, 4, 128, 512), f32)
    for l in range(L):
        for h in range(4):
            qh = wuq[l][:, h * 96:(h + 1) * 96]
            perm96 = np.arange(96)
            perm96[64:96] = 64 + p32
            qhp = qh[:, perm96]
            cw[l, h, :, 0:96] = qh[0:128]
            cw[l, h, :, 96:192] = qhp[0:128]
            cw[l, h, 0:64, 192:288] = qh[128:192]
            cw[l, h, 0:64, 288:384] = qhp[128:192]
            cw[l, h, :, 384:448] = wukv[l][:, h * 128:h * 128 + 64]
            cw[l, h, :, 448:512] = wukv[l][:, h * 128 + 64:h * 128 + 128]
    norms = np.stack([np.asarray(inp["attn_norm"], f32)[0], np.asarray(inp["attn_norm"], f32)[1],
                      np.asarray(inp["ffn_norm"], f32)[0], np.asarray(inp["ffn_norm"], f32)[1],
                      np.asarray(inp["final_norm"], f32)], axis=0)
    lamv = np.concatenate([np.asarray(inp[k], f32) for k in ("a_lambda_q1", "a_lambda_k1", "a_lambda_q2", "a_lambda_k2")], axis=1)
    pvec = np.zeros((128, 2 * PV_L + 6), f32)
    for b_ in range(4):
        pvec[32 * b_:32 * b_ + 32, 2 * PV_L + b_] = 1.0
    for b_ in range(2):
        pvec[64 * b_:64 * b_ + 64, 2 * PV_L + 4 + b_] = 1.0
    p64 = _partner_perm(64)
    for l in range(L):
        o = l * PV_L
        dqn = np.asarray(inp["d_q_norm"], f32)[l]
        dkn = np.asarray(inp["d_k_norm"], f32)[l]
        pvec[:, o + PV_DQ] = np.tile(dqn, 2)
        pvec[:, o + PV_DQP] = np.tile(dqn[p64], 2)
        pvec[:, o + PV_DK] = np.tile(dkn, 2)
        pvec[:, o + PV_DKP] = np.tile(dkn[p64], 2)
        pvec[:, o + PV_SUB] = np.tile(np.asarray(inp["a_subln"], f32)[l], 2)
        cqn = np.asarray(inp["c_q_norm"], f32)[l]
        pvec[:, o + PV_CQ0] = cqn[0:128]
        pvec[0:64, o + PV_CQ1] = cqn[128:192]
        pvec[:, o + PV_CKV] = np.asarray(inp["c_kv_norm"], f32)[l]
        cwt = np.asarray(inp["ffn_conv_w"], f32)[l].reshape(3, NFC, 128)
        cbt = np.asarray(inp["ffn_conv_b"], f32)[l].reshape(NFC, 128)
        for fc in range(NFC):
            for k in range(3):
                pvec[:, o + PV_CONV + 4 * fc + k] = cwt[k, fc]
            pvec[:, o + PV_CONV + 4 * fc + 3] = cbt[fc]
    t = np.arange(S, dtype=f32)
    inv = (np.float32(10000.0) ** (-np.arange(0, 32, 2, dtype=f32) / np.float32(32))).astype(f32)
    p = np.arange(128)
    fi = p % 16
    sgn = np.where((p % 32) < 16, -1.0, 1.0).astype(f32)
    angA = (t[None, :] * inv[fi][:, None]).astype(f32)
    pos_row = (np.arange(S) // 64).astype(f32)
    pos_col = (np.arange(S) % 64).astype(f32)
    posD = np.where(((p % 64) < 32)[:, None], pos_row[None, :], pos_col[None, :]).astype(f32)
    angD = (posD * inv[fi][:, None]).astype(f32)
    rope = np.stack([np.cos(angA), np.sin(angA) * sgn[:, None], np.cos(angD), np.sin(angD) * sgn[:, None]], axis=0).astype(f32)
    rpb = np.asarray(inp["b_rpb"], f32)
    nbias = np.full((L, 4, 128, 9, 5, 64), -30000.0, f32)
    kc = np.arange(64)[:, None]
    qc = np.arange(64)[None, :]
    cs = np.clip(qc - 8, 0, 48)
    colmask = (kc >= cs) & (kc < cs + 16)
    dci = np.clip(kc - qc + 15, 0, 30)
    for ci, qr in enumerate(_NAT_CLS_REP):
        cls, rs, mstart, ns = _nat_row_info(qr)
        assert cls == ci
        for s_ in range(ns):
            for b in range(2):
                kr = 2 * (mstart + s_) + b
                if kr < rs or kr >= rs + 8:
                    continue
                dr = kr - qr + 7
                vals = rpb[:, :, dr][:, :, dci]
                nbias[:, :, 64 * b:64 * b + 64, ci, s_, :] = np.where(colmask[None, None], vals, np.float32(-30000.0))
    nbias = nbias.reshape(L, 4, 128, 9 * 320)
    cst = np.zeros((3, 128, 128), f32)
    cst[0] = np.eye(128, dtype=f32)
    cst[1] = 1.0
    cst[2, 0:64, 0:64] = 1.0
    cst[2, 64:128, 64:128] = 1.0
    return {
        "wmix": wmix, "wbr": np.ascontiguousarray(wbr2), "wout": np.asarray(inp["w_out"], f32),
        "wfi": np.ascontiguousarray(wfi2), "wfo": np.asarray(inp["w_ffn_out"], f32), "cw": cw,
        "norms": np.ascontiguousarray(norms), "lamv": np.ascontiguousarray(lamv), "pvec": pvec,
        "rope": rope, "nbias": np.ascontiguousarray(nbias), "cst": cst,
    }


def kernel(**inputs):
    xp = np.asarray(inputs["x_prompt"], np.float32)
    xs = np.asarray(inputs["x_sample"], np.float32)
    shared = _prep_shared(inputs)
    nc = build_nc(SEQ_PER_CORE)
    in_maps = []
    for c in range(N_CORES):
        xc = np.concatenate([xp[4 * c:4 * c + 4], xs[c:c + 1]], axis=0)
        m = dict(shared)
        m["x"] = np.ascontiguousarray(xc)
        in_maps.append(m)
    res = run_bass_kernel_spmd(nc, in_maps, core_ids=list(range(N_CORES)))
    yp = np.empty_like(xp)
    ys = np.empty_like(xs)
    for c in range(N_CORES):
        y = res.results[c]["y"]
        yp[4 * c:4 * c + 4] = y[0:4]
        ys[c] = y[4]
    return (yp, ys)
```
